# Optimizing a Trainium2 kernel written in Bass

```python
import numpy as np
import jax
import jax.numpy as jnp
from jax import lax

D_MODEL = 2048
BATCH = 8
SEQ = 2048
DEPTH = 4

CTX_LEN = 256
GRID_W = 64

RW_HEADS = 16
RW_HEAD_DIM = 64
RW_WIDTH = RW_HEADS * RW_HEAD_DIM
DECAY_LORA = 64
ICLR_LORA = 64
GATE_LORA = 160
GN_EPS = 64e-5
NA_HEADS = 16
NA_HEAD_DIM = 64
NA_WIDTH = NA_HEADS * NA_HEAD_DIM
NA_WIN_ROWS = 8
NA_WIN_COLS = 16
NA_QCB = 16
NA_KCB = 32
SC_WIDTH = 1024
SC_TAPS = 3
N_BRANCH = 3
RW_COLS = 3 * RW_WIDTH + 2 * DECAY_LORA + 2 * ICLR_LORA + GATE_LORA
NA_IN_COLS = 3 * NA_WIDTH
SC_IN_COLS = 3 * SC_WIDTH
GATE_COLS = N_BRANCH * D_MODEL
IN_COLS = RW_COLS + NA_IN_COLS + SC_IN_COLS + GATE_COLS
N_EXPERTS = 32
TOP_K = 4
D_EXPERT = 512
SWIGLU_ALPHA = 1.702
SWIGLU_LIMIT = 7.0
MOE_BLOCK = 256
ROPE_BASE = 10000.0
LN_EPS = 1e-6
NEG_INF = -1e30
DEEPNORM_ALPHA = (2 * DEPTH) ** 0.25
DEEPNORM_BETA = (8 * DEPTH) ** -0.25

kernel_name = 'hybrid_rwkv7_natten_shortconv_moe_dit'


def _layernorm(x):
    x32 = x.astype(jnp.float32)
    mu = jnp.mean(x32, -1, keepdims=True)
    var = jnp.mean(jnp.square(x32 - mu), -1, keepdims=True)
    return ((x32 - mu) * lax.rsqrt(var + LN_EPS)).astype(x.dtype)


def _shift(u, direction):
    pad = jnp.zeros_like(u[:, :1])
    if direction > 0:
        return jnp.concatenate([pad, u[:, :-1]], axis=1)
    return jnp.concatenate([u[:, 1:], pad], axis=1)


def _modulation(cvec, w_mod, b_mod):
    m = jax.nn.silu(cvec) @ w_mod + b_mod
    return jnp.split(m, 6, axis=-1)


def _axial_rope(x):
    T, hd = x.shape[1], x.shape[-1]
    n_freq = hd // 4
    t = jnp.arange(T)
    inv = jnp.power(ROPE_BASE, -jnp.arange(n_freq, dtype=jnp.float32) / n_freq)
    row = (t // GRID_W).astype(jnp.float32)
    col = (t % GRID_W).astype(jnp.float32)
    ang = jnp.concatenate([row[:, None] * inv, col[:, None] * inv], -1)
    shape = (1, T) + (1,) * (x.ndim - 3) + (hd // 2,)
    cos = jnp.cos(ang).reshape(shape).astype(x.dtype)
    sin = jnp.sin(ang).reshape(shape).astype(x.dtype)
    x1, x2 = x[..., :hd // 2], x[..., hd // 2:]
    return jnp.concatenate([x1 * cos - x2 * sin, x2 * cos + x1 * sin], -1)


def _rwkv_prepare(p, mu, w0, w_lora, a0, a_lora, g_lora, k_k, k_a, rotate):
    p = p.astype(jnp.float32)
    B, T, _ = p.shape
    C, H, N = RW_WIDTH, RW_HEADS, RW_HEAD_DIM
    p = p + mu[0] * (_shift(p, 1) - p) + mu[1] * (_shift(p, -1) - p)
    r, k, v = p[..., :C], p[..., C:2 * C], p[..., 2 * C:3 * C]
    o = 3 * C
    wl = p[..., o:o + 2 * DECAY_LORA].reshape(B, T, 2, DECAY_LORA)
    o += 2 * DECAY_LORA
    al = p[..., o:o + 2 * ICLR_LORA].reshape(B, T, 2, ICLR_LORA)
    o += 2 * ICLR_LORA
    gl = p[..., o:o + GATE_LORA]
    w_raw = w0 + jnp.einsum('btdr,drc->btdc', jnp.tanh(wl), w_lora)
    decay = jnp.exp(-jnp.exp(-jax.nn.softplus(-w_raw) - 0.5))
    a = jax.nn.sigmoid(a0 + jnp.einsum('btdr,drc->btdc', al, a_lora))
    g = jax.nn.sigmoid(gl) @ g_lora
    kk = (k * k_k).reshape(B, T, H, N)
    kk = kk / jnp.maximum(jnp.sqrt(jnp.sum(kk * kk, -1, keepdims=True)), 1e-12)
    kd = (k[:, :, None, :] * (1.0 + (a - 1.0) * k_a)).reshape(B, T, 2, H, N)
    r = r.reshape(B, T, H, N)
    v = v.reshape(B, T, H, N)
    decay = decay.reshape(B, T, 2, H, N)
    a = a.reshape(B, T, 2, H, N)
    if rotate:
        r_s, kd_s, kk_s = _axial_rope(r), _axial_rope(kd), _axial_rope(kk)
    else:
        r_s, kd_s, kk_s = r, kd, kk
    scan_f = (r_s, decay[:, :, 0], kd_s[:, :, 0], v, kk_s, a[:, :, 0])
    scan_b = (r_s, decay[:, :, 1], kd_s[:, :, 1], v, kk_s, a[:, :, 1])
    return scan_f, scan_b, r, jnp.sum(kd, axis=2), v, g


def _wkv_scan(state0, r, w, k, v, kk, a, reverse):
    xs = tuple(jnp.moveaxis(t, 1, 0) for t in (r, w, k, v, kk, a))

    def step(S, inp):
        r_t, w_t, k_t, v_t, kk_t, a_t = inp
        s_kk = jnp.einsum('bhvk,bhk->bhv', S, kk_t)
        S = (S * w_t[:, :, None, :] - s_kk[..., None] * (kk_t * a_t)[:, :, None, :]
             + v_t[..., None] * k_t[:, :, None, :])
        return S, jnp.einsum('bhvk,bhk->bhv', S, r_t)

    S, out = lax.scan(step, state0, xs, reverse=reverse)
    return S, jnp.moveaxis(out, 0, 1)


def _rwkv_out(wkv, r, k_sum, v, g, r_k, gn_w, gn_b, out_dtype):
    B, T, H, N = wkv.shape
    mu = jnp.mean(wkv, -1, keepdims=True)
    var = jnp.mean(jnp.square(wkv - mu), -1, keepdims=True)
    y = ((wkv - mu) * lax.rsqrt(var + GN_EPS)).reshape(B, T, H * N) * gn_w + gn_b
    bonus = jnp.sum(r * k_sum * r_k, -1, keepdims=True) * v
    return ((y + bonus.reshape(B, T, H * N)) * g).astype(out_dtype)


def _na_heads(p):
    B, T, _ = p.shape
    return [p[..., i * NA_WIDTH:(i + 1) * NA_WIDTH].reshape(B, T, NA_HEADS, NA_HEAD_DIM) for i in range(3)]


def _na_static():
    n_cb = GRID_W // NA_QCB
    qcol = np.arange(GRID_W).reshape(n_cb, NA_QCB)
    kstart = np.clip(np.arange(n_cb) * NA_QCB - NA_WIN_COLS // 2, 0, GRID_W - NA_KCB)
    kcol = kstart[:, None] + np.arange(NA_KCB)[None, :]
    wstart = np.clip(qcol - NA_WIN_COLS // 2, 0, GRID_W - NA_WIN_COLS)
    inside = ((kcol[:, None, :] >= wstart[:, :, None])
              & (kcol[:, None, :] < wstart[:, :, None] + NA_WIN_COLS))
    dc = np.clip(kcol[:, None, :] - qcol[:, :, None] + NA_WIN_COLS - 1, 0, 2 * NA_WIN_COLS - 2)
    return kcol, inside, dc


def _na_latent(q, k, v, kc, vc, rpb):
    B, S, H, hd = q.shape
    rows = S // GRID_W
    kr = min(NA_WIN_ROWS, rows)
    n_cb = GRID_W // NA_QCB
    kcol, inside, dc = _na_static()
    scale = hd ** -0.5
    qr = jnp.moveaxis(q.reshape(B, rows, n_cb, NA_QCB, H, hd), 1, 0)
    kg = k.reshape(B, rows, GRID_W, H, hd)
    vg = v.reshape(B, rows, GRID_W, H, hd)
    r_idx = jnp.arange(rows)
    r_start = jnp.clip(r_idx - kr // 2, 0, rows - kr)

    def one_row(args):
        qb, r, rs = args
        kb = lax.dynamic_slice_in_dim(kg, rs, kr, axis=1)[:, :, kcol]
        vb = lax.dynamic_slice_in_dim(vg, rs, kr, axis=1)[:, :, kcol]
        s_win = jnp.einsum('bnqhd,bjnkhd->bhnqjk', qb, kb).astype(jnp.float32) * scale
        dr = rs + jnp.arange(kr) - r + NA_WIN_ROWS - 1
        bias = rpb[:, dr[:, None, None, None], dc[None]]
        bias = jnp.transpose(bias, (0, 2, 3, 1, 4)).astype(jnp.float32)
        s_win = jnp.where(inside[:, :, None, :], s_win + bias, NEG_INF)
        s_ctx = jnp.einsum('bnqhd,blhd->bhnql', qb, kc).astype(jnp.float32) * scale
        s = jnp.concatenate([s_win.reshape(B, H, n_cb, NA_QCB, kr * NA_KCB), s_ctx], -1)
        p = jax.nn.softmax(s, axis=-1).astype(v.dtype)
        p_win = p[..., :kr * NA_KCB].reshape(B, H, n_cb, NA_QCB, kr, NA_KCB)
        return (jnp.einsum('bhnqjk,bjnkhd->bnqhd', p_win, vb)
                + jnp.einsum('bhnql,blhd->bnqhd', p[..., kr * NA_KCB:], vc))

    out = lax.map(one_row, (qr, r_idx, r_start))
    return jnp.moveaxis(out, 0, 1).reshape(B, S, H * hd)


def _dense_attn(q, k, v):
    B, L, H, hd = q.shape
    s = jnp.einsum('bqhd,bkhd->bhqk', q, k).astype(jnp.float32) * hd ** -0.5
    p = jax.nn.softmax(s, axis=-1).astype(v.dtype)
    return jnp.einsum('bhqk,bkhd->bqhd', p, v).reshape(B, L, H * hd)


def _short_conv(p, conv_w):
    C = SC_WIDTH
    bg, cg, xin = p[..., :C], p[..., C:2 * C], p[..., 2 * C:]
    u = cg * xin
    y = conv_w[0] * _shift(u, 1) + conv_w[1] * u + conv_w[2] * _shift(u, -1)
    return bg * y


def _merge(y_rw, y_na, y_sc, gate_pre, b_gate, w_out_rw, w_out_na, w_out_sc, w_merge):
    D = w_merge.shape[0]
    gates = jax.nn.sigmoid(gate_pre + b_gate)
    m = (gates[..., :D] * (y_rw @ w_out_rw)
         + gates[..., D:2 * D] * (y_na @ w_out_na)
         + gates[..., 2 * D:] * (y_sc @ w_out_sc))
    return m @ w_merge


def _moe(h, router_w, router_b, w1, b1, w2, b2):
    N, D = h.shape
    logits = (h @ router_w + router_b).astype(jnp.float32)
    top_v, top_i = lax.top_k(logits, TOP_K)
    gate = jax.nn.softmax(top_v, axis=-1)
    A = N * TOP_K
    flat_e = top_i.reshape(A)
    flat_tok = jnp.arange(A, dtype=jnp.int32) // TOP_K
    order = jnp.argsort(flat_e)
    e_sorted = flat_e[order]
    counts = jnp.zeros(N_EXPERTS, jnp.int32).at[flat_e].add(1)
    padded = (counts + MOE_BLOCK - 1) // MOE_BLOCK * MOE_BLOCK
    pad_end = jnp.cumsum(padded)
    pad_start = pad_end - padded
    start = jnp.cumsum(counts) - counts
    dest = pad_start[e_sorted] + jnp.arange(A, dtype=jnp.int32) - start[e_sorted]
    n_blocks = -(-A // MOE_BLOCK) + N_EXPERTS
    n_slots = n_blocks * MOE_BLOCK
    slot_tok = jnp.zeros(n_slots, jnp.int32).at[dest].set(flat_tok[order])
    slot_w = jnp.zeros(n_slots, jnp.float32).at[dest].set(gate.reshape(A)[order])
    block_e = jnp.minimum(jnp.searchsorted(pad_end, jnp.arange(n_blocks, dtype=jnp.int32) * MOE_BLOCK, side='right'),
                          N_EXPERTS - 1)

    def expert_block(args):
        tok, e = args
        z = h[tok] @ w1[e] + b1[e]
        z_glu = jnp.minimum(z[:, ::2], SWIGLU_LIMIT)
        z_lin = jnp.clip(z[:, 1::2], -SWIGLU_LIMIT, SWIGLU_LIMIT)
        act = z_glu * jax.nn.sigmoid(SWIGLU_ALPHA * z_glu) * (z_lin + 1.0)
        return act @ w2[e] + b2[e]

    y = lax.map(expert_block, (slot_tok.reshape(n_blocks, MOE_BLOCK), block_e))
    y = y.reshape(n_slots, D) * slot_w[:, None].astype(y.dtype)
    return jax.ops.segment_sum(y, slot_tok, num_segments=N)


def _layer(x, xc, mods_lat, mods_ctx, lp, need_ctx_out):
    B, S, D = x.shape
    L = xc.shape[1]
    sh1, sc1, g1, sh2, sc2, g2 = mods_lat
    csh1, csc1, cg1, csh2, csc2, cg2 = mods_ctx
    h = _layernorm(x) * (1.0 + sc1) + sh1
    hc = _layernorm(xc) * (1.0 + csc1) + csh1
    proj = jnp.concatenate([hc, h], axis=1) @ lp['w_in']
    pc, pl = proj[:, :L], proj[:, L:]
    o_na = RW_COLS
    o_sc = RW_COLS + NA_IN_COLS
    o_gate = RW_COLS + NA_IN_COLS + SC_IN_COLS
    rw = (lp['rw_mu'], lp['rw_w0'], lp['rw_w_lora'], lp['rw_a0'], lp['rw_a_lora'],
          lp['rw_g_lora'], lp['rw_k_k'], lp['rw_k_a'])
    rw_post = (lp['rw_r_k'], lp['rw_gn_w'], lp['rw_gn_b'], x.dtype)
    fc = _rwkv_prepare(pc[..., :o_na], *rw, rotate=False)
    fl = _rwkv_prepare(pl[..., :o_na], *rw, rotate=True)
    s0 = jnp.zeros((B, RW_HEADS, RW_HEAD_DIM, RW_HEAD_DIM), jnp.float32)
    s_cf, wkv_cf = _wkv_scan(s0, *fc[0], reverse=False)
    s_cb, wkv_cb = _wkv_scan(s0, *fc[1], reverse=True)
    _, wkv_lf = _wkv_scan(s_cf, *fl[0], reverse=False)
    _, wkv_lb = _wkv_scan(s_cb, *fl[1], reverse=True)
    y_rw = _rwkv_out(wkv_lf + wkv_lb, *fl[2:], *rw_post)
    qc, kc, vc = _na_heads(pc[..., o_na:o_sc])
    q, k, v = _na_heads(pl[..., o_na:o_sc])
    y_na = _na_latent(q, k, v, kc, vc, lp['na_rpb'])
    y_sc = _short_conv(pl[..., o_sc:o_gate], lp['sc_conv'])
    merge_w = (lp['b_gate'], lp['w_out_rw'], lp['w_out_na'], lp['w_out_sc'], lp['w_merge'])
    mix = _merge(y_rw, y_na, y_sc, pl[..., o_gate:], *merge_w)
    x = _layernorm(DEEPNORM_ALPHA * x + g1 * mix) * lp['ln1_g'] + lp['ln1_b']
    moe_w = (lp['router_w'], lp['router_b'], lp['moe_w1'], lp['moe_b1'], lp['moe_w2'], lp['moe_b2'])
    h2 = _layernorm(x) * (1.0 + sc2) + sh2
    if need_ctx_out:
        yc_rw = _rwkv_out(wkv_cf + wkv_cb, *fc[2:], *rw_post)
        yc_na = _dense_attn(qc, kc, vc)
        yc_sc = _short_conv(pc[..., o_sc:o_gate], lp['sc_conv'])
        mix_c = _merge(yc_rw, yc_na, yc_sc, pc[..., o_gate:], *merge_w)
        xc = _layernorm(DEEPNORM_ALPHA * xc + cg1 * mix_c) * lp['ln1_g'] + lp['ln1_b']
        h2c = _layernorm(xc) * (1.0 + csc2) + csh2
        y = _moe(jnp.concatenate([h2c, h2], axis=1).reshape(B * (L + S), D), *moe_w).reshape(B, L + S, D)
        xc = _layernorm(DEEPNORM_ALPHA * xc + cg2 * y[:, :L]) * lp['ln2_g'] + lp['ln2_b']
        y_lat = y[:, L:]
    else:
        xc = None
        y_lat = _moe(h2.reshape(B * S, D), *moe_w).reshape(B, S, D)
    x = _layernorm(DEEPNORM_ALPHA * x + g2 * y_lat) * lp['ln2_g'] + lp['ln2_b']
    return x, xc


def setup_inputs(seed: int = 0) -> dict:
    key = jax.random.key(seed)
    ks = iter(jax.random.split(key, 48))

    def nrm(shape, s):
        return jax.random.normal(next(ks), shape, jnp.float32) * s

    def unif(shape, lo, hi):
        return jax.random.uniform(next(ks), shape, jnp.float32, lo, hi)

    Ld, D, C = DEPTH, D_MODEL, RW_WIDTH
    return {
        'x': nrm((BATCH, SEQ, D), 1.0),
        'c': nrm((BATCH, D), 1.0),
        'ctx': nrm((BATCH, CTX_LEN, D), 1.0),
        'c_ctx': nrm((D,), 1.0),
        'w_mod': nrm((Ld, D, 6 * D), 0.3 * D ** -0.5),
        'b_mod': nrm((Ld, 6 * D), 0.01),
        'w_in': nrm((Ld, D, IN_COLS), D ** -0.5),
        'rw_mu': unif((Ld, 2, RW_COLS), 0.0, 0.5),
        'rw_w0': unif((Ld, 2, C), -4.0, 1.0),
        'rw_w_lora': nrm((Ld, 2, DECAY_LORA, C), 0.5 * DECAY_LORA ** -0.5),
        'rw_a0': nrm((Ld, 2, C), 0.1),
        'rw_a_lora': nrm((Ld, 2, ICLR_LORA, C), 0.5 * ICLR_LORA ** -0.5),
        'rw_g_lora': nrm((Ld, GATE_LORA, C), GATE_LORA ** -0.5),
        'rw_k_k': 0.85 + nrm((Ld, C), 0.05),
        'rw_k_a': 1.0 + nrm((Ld, C), 0.05),
        'rw_r_k': nrm((Ld, RW_HEADS, RW_HEAD_DIM), 0.1),
        'rw_gn_w': 1.0 + nrm((Ld, C), 0.05),
        'rw_gn_b': nrm((Ld, C), 0.01),
        'w_out_rw': nrm((Ld, C, D), C ** -0.5),
        'na_rpb': nrm((Ld, NA_HEADS, 2 * NA_WIN_ROWS - 1, 2 * NA_WIN_COLS - 1), 0.1),
        'w_out_na': nrm((Ld, NA_WIDTH, D), NA_WIDTH ** -0.5),
        'sc_conv': nrm((Ld, SC_TAPS, SC_WIDTH), SC_TAPS ** -0.5),
        'w_out_sc': nrm((Ld, SC_WIDTH, D), SC_WIDTH ** -0.5),
        'b_gate': nrm((Ld, GATE_COLS), 0.01),
        'w_merge': nrm((Ld, D, D), DEEPNORM_BETA * D ** -0.5),
        'ln1_g': 1.0 + nrm((Ld, D), 0.05),
        'ln1_b': nrm((Ld, D), 0.01),
        'router_w': nrm((Ld, D, N_EXPERTS), D ** -0.5),
        'router_b': nrm((Ld, N_EXPERTS), 0.01),
        'moe_w1': nrm((Ld, N_EXPERTS, D, 2 * D_EXPERT), D ** -0.5),
        'moe_b1': nrm((Ld, N_EXPERTS, 2 * D_EXPERT), 0.01),
        'moe_w2': nrm((Ld, N_EXPERTS, D_EXPERT, D), DEEPNORM_BETA * D_EXPERT ** -0.5),
        'moe_b2': nrm((Ld, N_EXPERTS, D), 0.01),
        'ln2_g': 1.0 + nrm((Ld, D), 0.05),
        'ln2_b': nrm((Ld, D), 0.01),
    }


def reference(x, c, ctx, c_ctx, w_mod, b_mod, w_in, rw_mu, rw_w0, rw_w_lora, rw_a0, rw_a_lora,
              rw_g_lora, rw_k_k, rw_k_a, rw_r_k, rw_gn_w, rw_gn_b, w_out_rw, na_rpb, w_out_na,
              sc_conv, w_out_sc, b_gate, w_merge, ln1_g, ln1_b, router_w, router_b, moe_w1,
              moe_b1, moe_w2, moe_b2, ln2_g, ln2_b):
    xc = ctx
    for i in range(DEPTH):
        lp = dict(w_in=w_in[i], rw_mu=rw_mu[i], rw_w0=rw_w0[i], rw_w_lora=rw_w_lora[i],
                  rw_a0=rw_a0[i], rw_a_lora=rw_a_lora[i], rw_g_lora=rw_g_lora[i],
                  rw_k_k=rw_k_k[i], rw_k_a=rw_k_a[i], rw_r_k=rw_r_k[i], rw_gn_w=rw_gn_w[i],
                  rw_gn_b=rw_gn_b[i], w_out_rw=w_out_rw[i], na_rpb=na_rpb[i],
                  w_out_na=w_out_na[i], sc_conv=sc_conv[i], w_out_sc=w_out_sc[i],
                  b_gate=b_gate[i], w_merge=w_merge[i], ln1_g=ln1_g[i], ln1_b=ln1_b[i],
                  router_w=router_w[i], router_b=router_b[i], moe_w1=moe_w1[i],
                  moe_b1=moe_b1[i], moe_w2=moe_w2[i], moe_b2=moe_b2[i],
                  ln2_g=ln2_g[i], ln2_b=ln2_b[i])
        mods_lat = [m[:, None, :] for m in _modulation(c, w_mod[i], b_mod[i])]
        mods_ctx = _modulation(c_ctx, w_mod[i], b_mod[i])
        x, xc = _layer(x, xc, mods_lat, mods_ctx, lp, i < DEPTH - 1)
    return x
```

```python
import numpy as np
from contextlib import ExitStack
import concourse.bass as bass
import concourse.mybir as mybir
from concourse.bass_utils import run_bass_kernel_spmd

F32 = mybir.dt.float32
BF16 = mybir.dt.bfloat16
ALU = mybir.AluOpType
AF = mybir.ActivationFunctionType
AX = mybir.AxisListType

NCORES = 8
BF_W = ('w_in', 'w_out_rw', 'w_out_na', 'w_out_sc', 'w_merge', 'moe_w1', 'moe_w2')
CH = 256


def make_cfg(D=2048, S=2048, NE=32, DEPTH=4, L=256):
    c = dict(D=D, S=S, NE=NE, DEPTH=DEPTH, L=L)
    c['T'] = L + S
    c['NT'] = c['T'] // 128
    c['C'] = 1024
    c['RW_COLS'] = 3 * 1024 + 2 * 64 + 2 * 64 + 160
    c['IN_COLS'] = c['RW_COLS'] + 3 * 1024 + 3 * 1024 + 3 * D
    c['F'] = 512
    c['alpha'] = (2 * DEPTH) ** 0.25
    return c


BIG = ['w_mod', 'w_in', 'w_out_rw', 'w_out_na', 'w_out_sc', 'w_merge', 'moe_w1', 'moe_w2',
       'rw_w_lora', 'rw_a_lora', 'rw_g_lora']
SMALL = ['b_mod', 'rw_mu', 'rw_w0', 'rw_a0', 'rw_k_k', 'rw_k_a', 'rw_r_k', 'rw_gn_w', 'rw_gn_b',
         'sc_conv', 'b_gate', 'ln1_g', 'ln1_b', 'router_b', 'ln2_g', 'ln2_b', 'router_w', 'moe_b1', 'moe_b2']


def big_shapes(c):
    D, NE = c['D'], c['NE']
    return {'w_mod': (D, 6 * D), 'w_in': (D, c['IN_COLS']), 'w_out_rw': (1024, D), 'w_out_na': (1024, D),
            'w_out_sc': (1024, D), 'w_merge': (D, D), 'moe_w1': (NE * D, 1024), 'moe_w2': (NE * 512, D),
            'rw_w_lora': (128, 1024), 'rw_a_lora': (128, 1024), 'rw_g_lora': (160, 1024)}


def pack_layout(c):
    off, o = {}, 0
    for k, (r, w) in big_shapes(c).items():
        n = r * w
        assert n % (8 * 1024) == 0, (k, r, w)
        assert r % 8 == 0, k
        off[k] = o
        o += n // 8 // 1024
    o = (o + CH - 1) // CH * CH
    return off, o


def small_layout(c):
    D, NE = c['D'], c['NE']
    sz = {'b_mod': 6 * D, 'rw_mu': 2 * c['RW_COLS'], 'rw_w0': 2048, 'rw_a0': 2048, 'rw_k_k': 1024, 'rw_k_a': 1024,
          'rw_r_k': 1024, 'rw_gn_w': 1024, 'rw_gn_b': 1024, 'sc_conv': 3072, 'b_gate': 3 * D, 'ln1_g': D,
          'ln1_b': D, 'router_b': NE, 'ln2_g': D, 'ln2_b': D, 'router_w': D * NE,
          'moe_b1': NE * 1024, 'moe_b2': NE * D}
    off, o = {}, 0
    for k in SMALL:
        off[k] = o
        o += sz[k]
    return off, sz, o


class Buf:
    __slots__ = ('t', 'w', 'rs', 'name')

    def __init__(self, t, name=''):
        self.t = t
        self.w = None
        self.rs = []
        self.name = name

    def __getitem__(self, k):
        return self.t[k]


ENGS = ('pe', 'dve', 'act', 'pool', 'sp')


class Prog:
    NPOOL = 12

    def __init__(self, nc, es):
        self.nc = nc
        self.es = es
        self.sem = {e: es.enter_context(nc.semaphore('s_' + e)) for e in ENGS}
        self.cnt = {e: 0 for e in ENGS}
        self.dq = {}
        for q in ('sp', 'pool', 'act'):
            self.dq[q] = dict(sems=[es.enter_context(nc.semaphore('d_%s%d' % (q, i))) for i in range(self.NPOOL)], n=0)
        self.ops = {e: [] for e in ENGS}
        self.seen = {e: {} for e in ENGS}
        self.pend = {e: [] for e in ENGS}
        self.latest = {}
        self.nblk = 0
        self.pool_dedicated = False

    def _need(self, eng, tok):
        if tok is None:
            return
        sem, val = tok[0], tok[1]
        sd = self.seen[eng]
        if sd.get(id(sem), 0) >= val:
            return
        sd[id(sem)] = val
        self.pend[eng].append((sem, val))

    def op(self, eng, fn, reads=(), writes=(), dma=False, pe_acc=False):
        for b in reads:
            if b.w is not None and not (pe_acc and b.w[2] == 'pe' and eng == 'pe'):
                self._need(eng, b.w)
        for b in writes:
            if b.w is not None and not (pe_acc and b.w[2] == 'pe'):
                self._need(eng, b.w)
            for r in b.rs:
                self._need(eng, r)
        if dma:
            q = self.dq[eng]
            j = q['n']
            q['n'] += 1
            sem = q['sems'][j % self.NPOOL]
            val = 16 * (j // self.NPOOL + 1)
            if j >= self.NPOOL:
                self._need(eng, (sem, val - 16))
            inc = 16
        else:
            self.cnt[eng] += 1
            sem, val, inc = self.sem[eng], self.cnt[eng], 1
        tok = (sem, val, eng if not dma else 'dma')
        self.latest[id(sem)] = (sem, val)
        waits = self.pend[eng]
        self.pend[eng] = []
        self.ops[eng].append((waits, fn, sem, inc))
        self.total = getattr(self, 'total', 0) + 1 + len(waits)
        for b in reads:
            b.rs.append(tok)
        for b in writes:
            b.w = tok
            b.rs = []
        return tok

    def barrier(self):
        toks = list(self.latest.values())
        for e in ENGS:
            if e == 'pool' and self.pool_dedicated:
                continue
            for (sem, val) in toks:
                self._need(e, (sem, val))

    def flush(self):
        nc = self.nc
        ops = self.ops
        pend = self.pend
        with nc.Block() as block:
            def mk(e):
                def body(h):
                    for (waits, fn, sem, inc) in ops[e]:
                        for (s, v) in waits:
                            h.wait_ge(s, v)
                        fn(h).then_inc(sem, inc)
                    for (s, v) in pend[e]:
                        h.wait_ge(s, v)
                return body
            if ops['pe'] or pend['pe']:
                block.tensor(mk('pe'))
            if ops['dve'] or pend['dve']:
                block.vector(mk('dve'))
            if ops['act'] or pend['act']:
                block.scalar(mk('act'))
            if ops['pool'] or pend['pool']:
                block.gpsimd(mk('pool'))
            if ops['sp'] or pend['sp']:
                block.sync(mk('sp'))
        self.ops = {e: [] for e in ENGS}
        self.pend = {e: [] for e in ENGS}

    def dma(self, q, out, in_, reads=(), writes=(), **kw):
        return self.op(q, lambda h: h.dma_start(out=out, in_=in_, **kw), reads, writes, dma=True)

    def mm(self, out, lhsT, rhs, start, stop, reads=(), writes=()):
        return self.op('pe', lambda h: h.matmul(out, lhsT, rhs, start=start, stop=stop), reads, writes,
                       pe_acc=not start)

    def tr(self, out, in_, ident, reads=(), writes=()):
        return self.op('pe', lambda h: h.transpose(out, in_, ident), reads, writes)

    def act(self, out, in_, func, reads=(), writes=(), **kw):
        return self.op('act', lambda h: h.activation(out, in_, func, **kw), reads, writes)


class Alloc:
    _uid = [0]

    def __init__(self, nc):
        self.nc = nc
        self.es = ExitStack()

    @property
    def n(self):
        return Alloc._uid[0]

    @n.setter
    def n(self, v):
        Alloc._uid[0] = v

    def sb(self, shape, dt=F32, name=None):
        self.n += 1
        t = self.es.enter_context(self.nc.sbuf_tensor('%s_%d' % (name or 'sb', self.n), list(shape), dt))
        return Buf(t, name)

    def ps(self, shape, dt=F32, name=None):
        self.n += 1
        t = self.es.enter_context(self.nc.psum_tensor('%s_%d' % (name or 'ps', self.n), list(shape), dt))
        return Buf(t, name)

    def close(self):
        self.es.close()


class _View:
    def __init__(self, b, i):
        self.b, self.i = b, i

    def __getitem__(self, k):
        return self.b.t[(k[0], self.i) + tuple(k[1:])]

    @property
    def w(self):
        return self.b.w

    @w.setter
    def w(self, v):
        self.b.w = v

    @property
    def rs(self):
        return self.b.rs

    @rs.setter
    def rs(self, v):
        self.b.rs = v


def hTf_out(hf, gi, f):
    return f


class K:
    def __init__(self, c, dbg=None, phases=None, NB=1, inject=()):
        self.c = c
        self.inject = inject
        self.NB = NB
        self.dbg = dbg or {}
        self.phases = phases
        nc = self.nc = bass.Bass("TRN2", target_bir_lowering=False)
        D, T, L, S, DEPTH = c['D'], c['T'], c['L'], c['S'], c['DEPTH']
        self.soff, self.ssz, self.NS = small_layout(c)
        dt = nc.dram_tensor
        self.x_in = dt("x_in", [NB * S, D], F32, kind="ExternalInput").ap()
        self.ctx_in = dt("ctx_in", [NB * L, D], F32, kind="ExternalInput").ap()
        self.cvec = dt("cvec", [NB * 2, D], F32, kind="ExternalInput").ap()
        self.small = dt("small", [DEPTH, self.NS], F32, kind="ExternalInput").ap()
        self.rpbp = dt("rpbp", [DEPTH * 16 * 15, 127], F32, kind="ExternalInput").ap()
        self.consts = dt("consts", [128, 1024], F32, kind="ExternalInput").ap()
        self.rope = dt("rope", [S, 64], F32, kind="ExternalInput").ap()
        self.out = dt("out", [NB * S, D], F32, kind="ExternalOutput").ap()
        self.W = {}
        self.Wb = {}
        for k, (r, w) in big_shapes(c).items():
            self.W[k] = dt("W_" + k, [DEPTH * r, w], F32, kind="ExternalInput")
            if k in BF_W:
                for l in range(DEPTH):
                    self.Wb[(l, k)] = dt("Wb_%s_%d" % (k, l), [r, w], BF16)
        self.xres = dt("xres", [T, D], F32).ap()
        self.proj = dt("proj", [T, c['IN_COLS']], F32).ap()
        self.modd = dt("modd", [2, 6 * D], F32).ap()
        self.y_rw = dt("y_rw", [T, 1024], F32).ap()
        self.y_na = dt("y_na", [T, 1024], F32).ap()
        self.y_sc = dt("y_sc", [T, 1024], F32).ap()
        self.mmix = dt("mmix", [T, D], F32).ap()
        self.h2 = dt("h2", [T, D], F32).ap()
        self.Vf = dt("Vf", [T, 1024], F32).ap()
        self.Vb = dt("Vb", [T, 1024], F32).ap()
        self.Gd = dt("Gd", [T, 1024], F32).ap()
        self.BSd = dt("BSd", [T, 16], F32).ap()
        self.wkvf = dt("wkvf", [T, 1024], F32).ap()
        self.wkvb = dt("wkvb", [T, 1024], F32).ap()
        self.SDf = dt("SDf", [64, 5 * 16 * T], F32)
        self.SDb = dt("SDb", [64, 5 * 16 * T], F32)
        self.inj = {k: dt("inj_" + k, [T, 1024], F32, kind="ExternalInput").ap() for k in self.inject}
        self.dbg_out = {}
        for k, shp in self.dbg.items():
            if k.startswith('_'):
                continue
            self.dbg_out[k] = dt("dbg_" + k, list(shp), F32, kind="ExternalOutput").ap()

    def load_w(self, q, dst, l, name, nk, c0, ncol, bf=True, k0=0):
        rows, width = big_shapes(self.c)[name]
        if bf:
            t, base = self.Wb[(l, name)], 0
        else:
            t, base = self.W[name], l * rows * width
        src = bass.AP(t, base + k0 * 128 * width + c0, [[width, 128], [128 * width, nk], [1, ncol]])
        self.P.dma(q, dst[:, 0:nk, 0:ncol], src, writes=[dst])

    def svec(self, l, name, o=0, n=None, parts=128):
        n = n if n is not None else self.ssz[name]
        return bass.AP(self.small.tensor, l * self.NS + self.soff[name] + o, [[0, parts], [1, n]])

    def build(self):
        nc, c = self.nc, self.c
        self.es = ExitStack()
        with self.es:
            P = self.P = Prog(nc, self.es)
            self.setup()
            for l in range(c['DEPTH']):
                self.convert_weights(l)
            for bi in range(self.NB):
                self.bi = bi
                self.load_batch(bi)
                for l in range(c['DEPTH']):
                    self.layer(l)
                self.store_batch(bi)
            self.finish()
        return nc

    def want(self, name):
        return self.phases is None or name in self.phases

    def setup(self):
        P, nc, c = self.P, self.nc, self.c
        self.cst = a0 = Alloc(nc)
        self.ident_f = a0.sb([128, 128], F32, 'identf')
        self.ident_b = a0.sb([128, 128], BF16, 'identb')
        self.rev_b = a0.sb([128, 128], BF16, 'revb')
        self.rev_f = a0.sb([128, 128], F32, 'revf')
        P.dma('sp', self.ident_f[:, :], self.consts[:, 0:128], writes=[self.ident_f])
        P.dma('sp', self.rev_f[:, :], self.consts[:, 128:256], writes=[self.rev_f])
        P.op('dve', lambda h: h.tensor_copy(self.ident_b[:, :], self.ident_f[:, :]), [self.ident_f], [self.ident_b])
        P.op('dve', lambda h: h.tensor_copy(self.rev_b[:, :], self.rev_f[:, :]), [self.rev_f], [self.rev_b])
        P.barrier()
        P.flush()

    def load_batch(self, bi):
        P, c = self.P, self.c
        L, S = c['L'], c['S']
        P.dma('sp', self.xres[0:L, :], self.ctx_in[bi * L:(bi + 1) * L, :])
        P.dma('sp', self.xres[L:, :], self.x_in[bi * S:(bi + 1) * S, :])
        P.barrier()

    def store_batch(self, bi):
        P, c = self.P, self.c
        L, S = c['L'], c['S']
        P.dma('sp', self.out[bi * S:(bi + 1) * S, :], self.xres[L:, :])
        P.barrier()
        P.flush()

    def convert_weights(self, l):
        P, nc, c = self.P, self.nc, self.c
        a = Alloc(nc)
        NBUF = 4
        FW = 2048
        stg = [a.sb([128, FW], F32, 'cv_in') for _ in range(NBUF)]
        stb = [a.sb([128, FW], BF16, 'cv_out') for _ in range(NBUF)]
        i = 0
        for name in BF_W:
            rows, width = big_shapes(c)[name]
            n = rows * width
            assert n % FW == 0
            n2 = n // FW
            src = bass.AP(self.W[name], l * n, [[FW, n2], [1, FW]])
            dst = bass.AP(self.Wb[(l, name)], 0, [[FW, n2], [1, FW]])
            r0 = 0
            while r0 < n2:
                nr = min(128, n2 - r0)
                s_, b_ = stg[i % NBUF], stb[i % NBUF]
                P.dma('sp', s_[0:nr, :], src[r0:r0 + nr, :], writes=[s_])
                k = i % 3
                if k == 0:
                    P.op('act', lambda h, s_=s_, b_=b_, nr=nr: h.copy(b_[0:nr, :], s_[0:nr, :]), [s_], [b_])
                else:
                    P.op('dve' if k == 1 else 'pool', lambda h, s_=s_, b_=b_, nr=nr: h.tensor_copy(b_[0:nr, :], s_[0:nr, :]), [s_], [b_])
                P.dma('pool', dst[r0:r0 + nr, :], b_[0:nr, :], reads=[b_])
                r0 += nr
                i += 1
        P.barrier()
        P.flush()
        a.close()

    def layer(self, l):
        if self.want('mod'):
            self.phase_mod(l)
        if self.want('inproj'):
            self.phase_inproj(l)
        last = (l == self.c['DEPTH'] - 1)
        for k in self.inject:
            self.P.dma('sp', getattr(self, k)[:, :], self.inj[k][:, :])
            self.P.barrier()
        self.t_lo = self.c['L'] // 128 if last else 0
        if self.want('conv'):
            self.phase_conv(l)
        if self.want('na'):
            self.phase_na(l)
        if self.want('rwkv'):
            self.phase_rwkv(l)
        if self.want('merge'):
            self.phase_merge(l)
        if self.want('post1'):
            self.phase_post1(l)
        if self.want('moe'):
            self.phase_moe(l)

    def dbg_dump(self, l, name, src_ap):
        if name in self.dbg_out and l == self.dbg.get('_layer', 0):
            self.P.barrier()
            self.P.dma('sp', self.dbg_out[name][:, :], src_ap)
            self.P.barrier()

    def seq_bounds(self, tt):
        LT, NT = self.c['L'] // 128, self.c['NT']
        return (tt == 0 or tt == LT), (tt == LT - 1 or tt == NT - 1)

    def load_shift(self, dst, tt, c0, ncol, shift):
        P = self.P
        first, last = self.seq_bounds(tt)
        t0 = tt * 128
        if shift == 0:
            P.dma('sp', dst[:, 0:ncol], self.proj[t0:t0 + 128, c0:c0 + ncol], writes=[dst])
        elif shift < 0:
            if first:
                P.op('dve', lambda h: h.memset(dst[:, 0:ncol], 0.0), [], [dst])
                P.dma('sp', dst[1:128, 0:ncol], self.proj[t0:t0 + 127, c0:c0 + ncol], writes=[dst])
            else:
                P.dma('sp', dst[:, 0:ncol], self.proj[t0 - 1:t0 + 127, c0:c0 + ncol], writes=[dst])
        else:
            if last:
                P.op('dve', lambda h: h.memset(dst[:, 0:ncol], 0.0), [], [dst])
                P.dma('sp', dst[0:127, 0:ncol], self.proj[t0 + 1:t0 + 128, c0:c0 + ncol], writes=[dst])
            else:
                P.dma('sp', dst[:, 0:ncol], self.proj[t0 + 1:t0 + 129, c0:c0 + ncol], writes=[dst])

    def phase_conv(self, l):
        P, nc, c = self.P, self.nc, self.c
        NT = c['NT']
        o_sc = c['RW_COLS'] + 3072
        a = Alloc(nc)
        cw = a.sb([128, 3, 1024], F32, 'convw')
        P.dma('sp', cw[:, :, :], bass.AP(self.small.tensor, l * self.NS + self.soff['sc_conv'], [[0, 128], [1024, 3], [1, 1024]]),
              writes=[cw])
        bufs = [[a.sb([128, 2048], F32, 'cv%d' % i) for i in range(3)] for _ in range(2)]
        bg = [a.sb([128, 1024], F32, 'bg') for _ in range(2)]
        u = [[a.sb([128, 1024], F32, 'u%d' % i) for i in range(3)] for _ in range(2)]
        for tt in range(self.t_lo, NT):
            bm, b0, bp = bufs[tt % 2]
            um, u0, up = u[tt % 2]
            g_ = bg[tt % 2]
            self.load_shift(bm, tt, o_sc + 1024, 2048, -1)
            self.load_shift(b0, tt, o_sc + 1024, 2048, 0)
            self.load_shift(bp, tt, o_sc + 1024, 2048, +1)
            P.dma('sp', g_[:, :], self.proj[tt * 128:(tt + 1) * 128, o_sc:o_sc + 1024], writes=[g_])
            P.op('pool', lambda h, bm=bm, um=um: h.tensor_tensor(um[:, :], bm[:, 0:1024], bm[:, 1024:2048], ALU.mult), [bm], [um])
            P.op('dve', lambda h, b0=b0, u0=u0: h.tensor_tensor(u0[:, :], b0[:, 0:1024], b0[:, 1024:2048], ALU.mult), [b0], [u0])
            P.op('pool', lambda h, bp=bp, up=up: h.tensor_tensor(up[:, :], bp[:, 0:1024], bp[:, 1024:2048], ALU.mult), [bp], [up])
            P.op('pool', lambda h, um=um: h.tensor_tensor(um[:, :], um[:, :], cw[:, 0, :], ALU.mult), [um, cw], [um])
            P.op('dve', lambda h, u0=u0: h.tensor_tensor(u0[:, :], u0[:, :], cw[:, 1, :], ALU.mult), [u0, cw], [u0])
            P.op('pool', lambda h, up=up: h.tensor_tensor(up[:, :], up[:, :], cw[:, 2, :], ALU.mult), [up, cw], [up])
            P.op('dve', lambda h, u0=u0, um=um: h.tensor_tensor(u0[:, :], u0[:, :], um[:, :], ALU.add), [u0, um], [u0])
            P.op('dve', lambda h, u0=u0, up=up: h.tensor_tensor(u0[:, :], u0[:, :], up[:, :], ALU.add), [u0, up], [u0])
            P.op('dve', lambda h, u0=u0, g_=g_: h.tensor_tensor(u0[:, :], u0[:, :], g_[:, :], ALU.mult), [u0, g_], [u0])
            P.dma('sp', self.y_sc[tt * 128:(tt + 1) * 128, :], u0[:, :], reads=[u0])
        self.dbg_dump(l, 'y_sc', self.y_sc[:, :])
        P.barrier()
        P.flush()
        a.close()

    def phase_na(self, l):
        P, nc, c = self.P, self.nc, self.c
        NT, T, L, S = c['NT'], c['T'], c['L'], c['S']
        LT = L // 128
        rows = S // 64
        HG = 4
        o_na = c['RW_COLS']
        a = Alloc(nc)
        mask = a.sb([64, 64], F32, 'namask')
        P.dma('sp', mask[:, :], self.consts[0:64, 256:320], writes=[mask])
        RB = a.sb([64, HG, 15, 64], F32, 'RB')
        rbr = [a.sb([64, 15, 64], F32, 'rbr') for _ in range(2)]
        QT = a.sb([128, HG // 2, T], BF16, 'QT')
        KT = a.sb([128, HG // 2, T], BF16, 'KT')
        VA = a.sb([128, NT, HG * 64], BF16, 'VA')
        VB = a.sb([128, NT - 1, HG * 64], BF16, 'VB')
        stg = [a.sb([128, 3, HG * 64], F32, 'nastg') for _ in range(2)]
        stgb = [a.sb([128, 2, HG * 64], BF16, 'nastgb') for _ in range(2)]
        stv = [a.sb([128, HG * 64], F32, 'nastv') for _ in range(2)]
        ptq = [a.ps([128, 4, 128], BF16, 'ptq') for _ in range(2)]
        sc = [a.sb([64, 768], F32, 'sc') for _ in range(2)]
        pb = [a.sb([64, 768], BF16, 'pb') for _ in range(2)]
        pT = [a.sb([128, 6, 64], BF16, 'pT') for _ in range(2)]
        st = [a.sb([64, 8], F32, 'nast') for _ in range(2)]
        orow = [a.sb([64, HG * 64], F32, 'orow') for _ in range(2)]
        ps_s = [a.ps([64, 512], F32, 'ps_s') for _ in range(2)]
        ps_c = [a.ps([64, 256], F32, 'ps_c') for _ in range(2)]
        pt = [a.ps([128, 6, 64], BF16, 'ptp') for _ in range(1)]
        po = [a.ps([64, 64], F32, 'po') for _ in range(1)]
        n = 0
        nrow = 0
        for g in range(16 // HG):
            import os as _os
            NAP = _os.environ.get('NA_PART', '')
            for hl in range(HG if NAP != 'qkv' else 0):
                h_ = g * HG + hl
                r_ = rbr[hl % 2]
                P.dma('sp', r_[:, :, :], bass.AP(self.rpbp.tensor, ((l * 16 + h_) * 15) * 127, [[1, 64], [127, 15], [1, 64]]),
                      writes=[r_])
                import os as _os
                rsrc = r_.t[:, :, ::-1] if not _os.environ.get('NA_NOREV') else r_.t[:, :, :]
                P.op('dve', lambda h, r_=r_, hl=hl, rsrc=rsrc: h.tensor_tensor(RB[:, hl, :, :], rsrc,
                                                                   mask[:, :].unsqueeze(1).to_broadcast([64, 15, 64]), ALU.add),
                     [r_, mask], [RB])
            for tt in range(NT if NAP != 'rb' else 0):
                s_, sb_ = stg[tt % 2], stgb[tt % 2]
                P.dma('sp', s_[:, :, :], bass.AP(self.proj.tensor, tt * 128 * c['IN_COLS'] + o_na + g * HG * 64,
                                                 [[c['IN_COLS'], 128], [1024, 3], [1, HG * 64]]), writes=[s_])
                QL = int(_os.environ.get('NA_Q', '9'))
                if QL < 2:
                    continue
                P.op('pool', lambda h, s_=s_, sb_=sb_: h.tensor_copy(sb_[:, :, :], s_[:, 0:2, :]), [s_], [sb_])
                P.op('pool', lambda h, s_=s_, tt=tt: h.tensor_copy(VA[:, tt, :], s_[:, 2, :]), [s_], [VA])
                if QL < 3:
                    continue
                p = ptq[tt % 2]
                for i in range(HG):
                    P.tr(p[:, i, :], sb_[:, i // (HG // 2), (i % (HG // 2)) * 128:(i % (HG // 2) + 1) * 128], self.ident_b[:, :],
                         [sb_, self.ident_b], [p])
                if QL < 4:
                    continue
                if tt % 2:
                    P.op('dve', lambda h, p=p, tt=tt: h.tensor_copy(QT[:, :, tt * 128:(tt + 1) * 128], p[:, 0:HG // 2, :]), [p], [QT])
                    P.op('dve', lambda h, p=p, tt=tt: h.tensor_copy(KT[:, :, tt * 128:(tt + 1) * 128], p[:, HG // 2:HG, :]), [p], [KT])
                else:
                    P.op('act', lambda h, p=p, tt=tt: h.copy(QT[:, :, tt * 128:(tt + 1) * 128], p[:, 0:HG // 2, :]), [p], [QT])
                    P.op('act', lambda h, p=p, tt=tt: h.copy(KT[:, :, tt * 128:(tt + 1) * 128], p[:, HG // 2:HG, :]), [p], [KT])
                if QL < 5:
                    continue
                if tt < NT - 1:
                    v_ = stv[tt % 2]
                    P.dma('sp', v_[:, :], self.proj[tt * 128 + 64:tt * 128 + 192, o_na + 2048 + g * HG * 64:o_na + 2048 + (g + 1) * HG * 64],
                          writes=[v_])
                    P.op('pool', lambda h, v_=v_, tt=tt: h.tensor_copy(VB[:, tt, :], v_[:, :]), [v_], [VB])
            units = []
            if self.t_lo == 0:
                for rc in range(L // 64):
                    units.append((rc * 64, None))
            for r in range(rows):
                rs = min(max(r - 4, 0), rows - 8)
                units.append((L + r * 64, (L + rs * 64, rs - r + 7)))
            import os as _os
            if _os.environ.get('NA_NOUNITS'):
                units = []
            for (q0, win) in units:
                ob = orow[nrow % 2]
                nrow += 1
                for hl in range(HG):
                    i = n % 2
                    n += 1
                    s_, p_, t_, st_ = sc[i], pb[i], pT[i], st[i]
                    pb0 = (hl % 2) * 64
                    chunks = []
                    if win is not None:
                        k0, dr0 = win
                        P.mm(ps_s[i][:, :], QT[pb0:pb0 + 64, hl // 2, q0:q0 + 64], KT[pb0:pb0 + 64, hl // 2, k0:k0 + 512], True, True, [QT, KT], [ps_s[i]])
                        P.op('dve', lambda h, i=i, s_=s_, hl=hl, dr0=dr0: h.scalar_tensor_tensor(
                            s_[:, 0:512], ps_s[i][:, :], 0.125, RB[:, hl, dr0:dr0 + 8, :].rearrange("p a b -> p (a b)"), ALU.mult, ALU.add),
                            [ps_s[i], RB], [s_])
                        for j in range(4):
                            cj = k0 + 128 * j
                            chunks.append((VA, cj // 128) if cj % 128 == 0 else (VB, (cj - 64) // 128))
                        co = 512
                    else:
                        co = 0
                    P.mm(ps_c[i][:, :], QT[pb0:pb0 + 64, hl // 2, q0:q0 + 64], KT[pb0:pb0 + 64, hl // 2, 0:L], True, True, [QT, KT], [ps_c[i]])
                    P.act(s_[:, co:co + L], ps_c[i][:, :], AF.Copy, [ps_c[i]], [s_], scale=0.125)
                    for j in range(L // 128):
                        chunks.append((VA, j))
                    ntot = co + L
                    P.op('dve', lambda h, s_=s_, st_=st_, ntot=ntot: h.reduce_max(st_[:, 0:1], s_[:, 0:ntot], AX.X), [s_], [st_])
                    P.op('dve', lambda h, st_=st_: h.tensor_scalar(st_[:, 1:2], st_[:, 0:1], -1.0, None, ALU.mult), [st_], [st_])
                    P.act(p_[:, 0:ntot], s_[:, 0:ntot], AF.Exp, [s_, st_], [p_, st_], bias=st_[:, 1:2], accum_out=st_[:, 2:3])
                    P.op('dve', lambda h, st_=st_: h.reciprocal(st_[:, 3:4], st_[:, 2:3]), [st_], [st_])
                    nch = len(chunks)
                    for j in range(nch):
                        P.tr(pt[0][:, j, :], p_[:, j * 128:(j + 1) * 128], self.ident_b[0:64, 0:64], [p_, self.ident_b], [pt[0]])
                    P.op('act', lambda h, t_=t_, nch=nch: h.copy(t_[:, 0:nch, :], pt[0][:, 0:nch, :]), [pt[0]], [t_])
                    for j, (vb_, ti) in enumerate(chunks):
                        P.mm(po[0][:, :], t_[:, j, :], vb_[:, ti, hl * 64:(hl + 1) * 64], j == 0, j == nch - 1, [t_, vb_], [po[0]])
                    P.op('dve', lambda h, ob=ob, hl=hl, st_=st_: h.tensor_scalar(ob[:, hl * 64:(hl + 1) * 64], po[0][:, :], st_[:, 3:4], None,
                                                                               ALU.mult), [po[0], st_], [ob])
                P.dma('sp', self.y_na[q0:q0 + 64, g * HG * 64:(g + 1) * HG * 64], ob[:, :], reads=[ob])
        self.dbg_dump(l, 'y_na', self.y_na[:, :])
        P.barrier()
        P.flush()
        a.close()

    def step_off(self, tt):
        c = self.c
        LT, NT, L = c['L'] // 128, c['NT'], c['L']
        if tt < LT:
            return tt * 128, (LT - 1 - tt) * 128
        return tt * 128, L + (NT - 1 - tt) * 128

    def phase_rwkv(self, l):
        self.rwkv_prep(l)
        self.rwkv_scan(l)
        self.rwkv_post(l)

    def rwkv_prep(self, l):
        P, nc, c = self.P, self.nc, self.c
        NT, T, L = c['NT'], c['T'], c['L']
        LT = L // 128
        RC = c['RW_COLS']
        a = Alloc(nc)
        so, NS = self.soff, self.NS

        def bc(name, o, n, nm):
            t = a.sb([128, n], F32, nm)
            P.dma('sp', t[:, :], self.svec(l, name, o=o, n=n), writes=[t])
            return t
        mu0 = bc('rw_mu', 0, RC, 'mu0')
        mu1 = bc('rw_mu', RC, RC, 'mu1')
        cf0 = a.sb([128, RC], F32, 'cf0')
        P.op('dve', lambda h: h.tensor_tensor(cf0[:, :], mu0[:, :], mu1[:, :], ALU.add), [mu0, mu1], [cf0])
        P.op('dve', lambda h: h.tensor_scalar(cf0[:, :], cf0[:, :], -1.0, 1.0, ALU.mult, ALU.add), [cf0], [cf0])
        w0 = bc('rw_w0', 0, 2048, 'w0')
        a0 = bc('rw_a0', 0, 2048, 'a0')
        k_k = bc('rw_k_k', 0, 1024, 'k_k')
        k_a = bc('rw_k_a', 0, 1024, 'k_a')
        r_k = bc('rw_r_k', 0, 1024, 'r_k')
        omka = a.sb([128, 1024], F32, 'omka')
        P.op('dve', lambda h: h.tensor_scalar(omka[:, :], k_a[:, :], -1.0, 1.0, ALU.mult, ALU.add), [k_a], [omka])
        wlo = a.sb([128, 1, 1024], F32, 'wlo')
        alo = a.sb([128, 1, 1024], F32, 'alo')
        glo = a.sb([128, 1, 1024], F32, 'glo')
        glo2 = a.sb([32, 1024], F32, 'glo2')
        self.load_w('sp', wlo, l, 'rw_w_lora', 1, 0, 1024, bf=False)
        self.load_w('sp', alo, l, 'rw_a_lora', 1, 0, 1024, bf=False)
        self.load_w('sp', glo, l, 'rw_g_lora', 1, 0, 1024, bf=False)
        P.dma('sp', glo2[:, :], bass.AP(self.W['rw_g_lora'], l * 160 * 1024 + 128 * 1024, [[1024, 32], [1, 1024]]), writes=[glo2])
        jf = a.sb([128, 128], F32, 'jfull')
        P.dma('sp', jf[:, :], self.consts[:, 320:448], writes=[jf])
        P0 = a.sb([128, RC], F32, 'P0')
        Pm = a.sb([128, RC], F32, 'Pm')
        Pp = a.sb([128, RC], F32, 'Pp')
        Y = [a.sb([128, 1024], F32, 'Y%d' % i) for i in range(7)]
        cs = a.sb([128, 64], F32, 'cs')
        lT = a.sb([128, 3, 128], F32, 'lT')
        lT2 = a.sb([32, 128], F32, 'lT2')
        sm = a.sb([128, 64], F32, 'sm')
        stg = [a.sb([128, 4, 128], F32, 'cmstg') for _ in range(3)]
        pT = a.ps([128, 4, 128], F32, 'pT')
        pm = [a.ps([128, 512], F32, 'pm') for _ in range(2)]
        pc = [a.ps([128, 4, 128], F32, 'pc') for _ in range(3)]
        r_, k_, v_ = (slice(0, 1024), slice(1024, 2048), slice(2048, 3072))
        ncm = 0
        for tt in range(NT):
            lat = tt >= LT
            s0f, s0b = self.step_off(tt)
            self.load_shift(Pm, tt, 0, RC, -1)
            self.load_shift(P0, tt, 0, RC, 0)
            self.load_shift(Pp, tt, 0, RC, +1)
            P.op('pool', lambda h: h.tensor_tensor(Pm[:, :], Pm[:, :], mu0[:, :], ALU.mult), [Pm, mu0], [Pm])
            P.op('pool', lambda h: h.tensor_tensor(Pp[:, :], Pp[:, :], mu1[:, :], ALU.mult), [Pp, mu1], [Pp])
            P.op('dve', lambda h: h.tensor_tensor(P0[:, :], P0[:, :], cf0[:, :], ALU.mult), [P0, cf0], [P0])
            P.op('dve', lambda h: h.tensor_tensor(P0[:, :], P0[:, :], Pm[:, :], ALU.add), [P0, Pm], [P0])
            P.op('dve', lambda h: h.tensor_tensor(P0[:, :], P0[:, :], Pp[:, :], ALU.add), [P0, Pp], [P0])
            P.dma('sp', self.Vf[tt * 128:(tt + 1) * 128, :], P0[:, v_], reads=[P0])
            for hf in range(2):
                P.mm(pm[hf][:, :], jf[:, :], P0[:, 2048 + hf * 512:2048 + (hf + 1) * 512], True, True, [jf, P0], [pm[hf]])
                P.op('dve', lambda h, hf=hf: h.tensor_copy(Y[6][:, hf * 512:(hf + 1) * 512], pm[hf][:, :]), [pm[hf]], [Y[6]])
            P.dma('sp', self.Vb[s0b:s0b + 128, :], Y[6][:, :], reads=[Y[6]])
            P.act(P0[:, 3072:3200], P0[:, 3072:3200], AF.Tanh, [P0], [P0])
            P.act(P0[:, 3328:3488], P0[:, 3328:3488], AF.Sigmoid, [P0], [P0])
            P.tr(pT[:, 0, :], P0[:, 3072:3200], self.ident_f[:, :], [P0, self.ident_f], [pT])
            P.tr(pT[:, 1, :], P0[:, 3200:3328], self.ident_f[:, :], [P0, self.ident_f], [pT])
            P.tr(pT[:, 2, :], P0[:, 3328:3456], self.ident_f[:, :], [P0, self.ident_f], [pT])
            P.tr(pT[0:32, 3, :], P0[:, 3456:3488], self.ident_f[:, :], [P0, self.ident_f], [pT])
            P.op('dve', lambda h: h.tensor_copy(lT[:, :, :], pT[:, 0:3, :]), [pT], [lT])
            P.op('dve', lambda h: h.tensor_copy(lT2[:, :], pT[0:32, 3, :]), [pT], [lT2])
            for d in range(2):
                for (src, lo, bias, dst, isw) in ((0, wlo, w0, Y[d], True), (1, alo, a0, Y[2 + d], False)):
                    for hf in range(2):
                        P.mm(pm[hf][:, :], lT[d * 64:(d + 1) * 64, src, :], lo[d * 64:(d + 1) * 64, 0, hf * 512:(hf + 1) * 512], True, True,
                             [lT, lo], [pm[hf]])
                        P.op('dve', lambda h, hf=hf, dst=dst, bias=bias, d=d: h.tensor_tensor(
                            dst[:, hf * 512:(hf + 1) * 512], pm[hf][:, :], bias[:, d * 1024 + hf * 512:d * 1024 + (hf + 1) * 512], ALU.add),
                            [pm[hf], bias], [dst])
                    P.act(dst[:, :], dst[:, :], AF.Sigmoid, [dst], [dst])
                    if isw:
                        P.act(dst[:, :], dst[:, :], AF.Exp, [dst], [dst], scale=-0.6065306597126334)
            for hf in range(2):
                P.mm(pm[hf][:, :], lT[:, 2, :], glo[:, 0, hf * 512:(hf + 1) * 512], True, False, [lT, glo], [pm[hf]])
                P.mm(pm[hf][:, :], lT2[:, :], glo2[:, hf * 512:(hf + 1) * 512], False, True, [lT2, glo2], [pm[hf]])
                P.op('dve', lambda h, hf=hf: h.tensor_copy(Y[6][:, hf * 512:(hf + 1) * 512], pm[hf][:, :]), [pm[hf]], [Y[6]])
            P.dma('sp', self.Gd[tt * 128:(tt + 1) * 128, :], Y[6][:, :], reads=[Y[6]])
            for d in range(2):
                dst = Pm[:, d * 1024:(d + 1) * 1024]
                P.op('dve', lambda h, d=d, dst=dst: h.tensor_tensor(dst, Y[2 + d][:, :], k_a[:, :], ALU.mult), [Y[2 + d], k_a], [Pm])
                P.op('dve', lambda h, dst=dst: h.tensor_tensor(dst, dst, omka[:, :], ALU.add), [Pm, omka], [Pm])
                P.op('dve', lambda h, dst=dst: h.tensor_tensor(dst, dst, P0[:, k_], ALU.mult), [Pm, P0], [Pm])
            P.op('dve', lambda h: h.tensor_tensor(Y[6][:, :], Pm[:, 0:1024], Pm[:, 1024:2048], ALU.add), [Pm], [Y[6]])
            P.op('dve', lambda h: h.tensor_tensor(Y[6][:, :], Y[6][:, :], P0[:, r_], ALU.mult), [Y[6], P0], [Y[6]])
            P.op('dve', lambda h: h.tensor_tensor(Y[6][:, :], Y[6][:, :], r_k[:, :], ALU.mult), [Y[6], r_k], [Y[6]])
            P.op('dve', lambda h: h.reduce_sum(sm[:, 0:16], Y[6][:, :].rearrange("p (h k) -> p h k", k=64), AX.X), [Y[6]], [sm])
            P.dma('sp', self.BSd[tt * 128:(tt + 1) * 128, :], sm[:, 0:16], reads=[sm])
            kk = Pp[:, 0:1024]
            P.op('dve', lambda h: h.tensor_tensor(kk, P0[:, k_], k_k[:, :], ALU.mult), [P0, k_k], [Pp])
            P.op('dve', lambda h: h.tensor_tensor(Y[6][:, :], kk, kk, ALU.mult), [Pp], [Y[6]])
            P.op('dve', lambda h: h.reduce_sum(sm[:, 16:32], Y[6][:, :].rearrange("p (h k) -> p h k", k=64), AX.X), [Y[6]], [sm])
            P.act(sm[:, 16:32], sm[:, 16:32], AF.Sqrt, [sm], [sm])
            P.op('dve', lambda h: h.tensor_scalar(sm[:, 16:32], sm[:, 16:32], 1e-12, None, ALU.max), [sm], [sm])
            P.op('dve', lambda h: h.reciprocal(sm[:, 32:48], sm[:, 16:32]), [sm], [sm])
            P.op('dve', lambda h: h.tensor_tensor(kk.rearrange("p (h k) -> p h k", k=64), kk.rearrange("p (h k) -> p h k", k=64),
                                                  sm[:, 32:48].unsqueeze(2).to_broadcast([128, 16, 64]), ALU.mult), [Pp, sm], [Pp])
            if lat:
                P.dma('sp', cs[:, :], self.rope[(tt - LT) * 128:(tt - LT + 1) * 128, :], writes=[cs])
                cosb = cs[:, 0:32].unsqueeze(1).to_broadcast([128, 16, 32])
                sinb = cs[:, 32:64].unsqueeze(1).to_broadcast([128, 16, 32])

                def rope(src_ap, src_buf, dst_ap, dst_buf):
                    x = src_ap.rearrange("p (h k) -> p h k", k=64)
                    y = dst_ap.rearrange("p (h k) -> p h k", k=64)
                    t = Y[6][:, 0:512].rearrange("p (h k) -> p h k", k=32)
                    P.op('dve', lambda h: h.tensor_tensor(y[:, :, 0:32], x[:, :, 0:32], cosb, ALU.mult), [src_buf, cs], [dst_buf])
                    P.op('dve', lambda h: h.tensor_tensor(t, x[:, :, 32:64], sinb, ALU.mult), [src_buf, cs], [Y[6]])
                    P.op('dve', lambda h: h.tensor_tensor(y[:, :, 0:32], y[:, :, 0:32], t, ALU.subtract), [dst_buf, Y[6]], [dst_buf])
                    P.op('dve', lambda h: h.tensor_tensor(y[:, :, 32:64], x[:, :, 32:64], cosb, ALU.mult), [src_buf, cs], [dst_buf])
                    P.op('dve', lambda h: h.tensor_tensor(t, x[:, :, 0:32], sinb, ALU.mult), [src_buf, cs], [Y[6]])
                    P.op('dve', lambda h: h.tensor_tensor(y[:, :, 32:64], y[:, :, 32:64], t, ALU.add), [dst_buf, Y[6]], [dst_buf])
                rope(P0[:, r_], P0, Y[4][:, :], Y[4])
                rope(Pp[:, 0:1024], Pp, Y[5][:, :], Y[5])
                rope(Pm[:, 0:1024], Pm, Pp[:, 1024:2048], Pp)
                rope(Pm[:, 1024:2048], Pm, Pp[:, 2048:3072], Pp)
                Rr, KKr, KD = (Y[4][:, :], Y[4]), (Y[5][:, :], Y[5]), [(Pp[:, 1024:2048], Pp), (Pp[:, 2048:3072], Pp)]
            else:
                Rr, KKr, KD = (P0[:, r_], P0), (Pp[:, 0:1024], Pp), [(Pm[:, 0:1024], Pm), (Pm[:, 1024:2048], Pm)]
            for d in range(2):
                P.op('dve', lambda h, d=d, kka=KKr[0]: h.tensor_tensor(Y[2 + d][:, :], Y[2 + d][:, :], kka, ALU.mult), [Y[2 + d], KKr[1]], [Y[2 + d]])
            for d in range(2):
                srcs = [KKr, (Y[2 + d][:, :], Y[2 + d]), KD[d], (Y[d][:, :], Y[d]), Rr]
                rhs = self.ident_f if d == 0 else jf
                s0 = s0f if d == 0 else s0b
                SD = self.SDf if d == 0 else self.SDb
                for q, (sap, sbuf) in enumerate(srcs):
                    for hh in range(2):
                        p = pc[ncm % 3]
                        st_ = stg[ncm % 3]
                        for j in range(4):
                            h0 = hh * 8 + 2 * j
                            P.mm(p[:, j, :], sap[:, h0 * 64:(h0 + 2) * 64], rhs[:, :], True, True, [sbuf, rhs], [p])
                        if ncm % 2:
                            P.op('act', lambda h, p=p, st_=st_: h.copy(st_[:, :, :], p[:, :, :]), [p], [st_])
                        else:
                            P.op('dve', lambda h, p=p, st_=st_: h.tensor_copy(st_[:, :, :], p[:, :, :]), [p], [st_])
                        ncm += 1
                        for par in range(2):
                            dst = bass.AP(SD, (q * 16 + hh * 8 + par) * T + s0, [[5 * 16 * T, 64], [2 * T, 4], [1, 128]])
                            P.dma('sp', dst, st_[par * 64:(par + 1) * 64, :, :], reads=[st_])
        P.barrier()
        P.flush()
        a.close()

    def rwkv_scan(self, l):
        P, nc, c = self.P, self.nc, self.c
        T = c['T']
        a = Alloc(nc)
        BO = a.sb([128, 128], F32, 'BO')
        SEL = a.sb([2, 128], F32, 'SEL')
        SEL2 = a.sb([128, 2], F32, 'SEL2')
        P.dma('sp', BO[:, :], self.consts[:, 448:576], writes=[BO])
        P.dma('sp', SEL[:, :], self.consts[0:2, 576:704], writes=[SEL])
        P.dma('sp', SEL2[:, :], self.consts[:, 704:706], writes=[SEL2])
        ST = a.sb([128, 1024], F32, 'ST')
        P.op('dve', lambda h: h.memset(ST[:, :], 0.0), [], [ST])
        BSZ = 64
        sd = [a.sb([128, 5, 16, BSZ], F32, 'sd') for _ in range(2)]
        VS = 4
        vr = [a.sb([2, VS, 1024], F32, 'vr') for _ in range(2)]
        ost = [a.sb([2, VS, 1024], F32, 'ost') for _ in range(2)]
        tmp = [a.sb([128, 1024], F32, 'tmp') for _ in range(2)]
        t2 = a.sb([128, 1024], F32, 't2')
        vbs = [a.sb([128, 1024], F32, 'vbs') for _ in range(2)]
        t3 = [a.sb([128, 1024], F32, 't3') for _ in range(2)]
        t4 = [a.sb([128, 1024], F32, 't4') for _ in range(2)]
        pskk = [a.ps([128, 512], F32, 'pskk') for _ in range(2)]
        pvb = [a.ps([128, 512], F32, 'pvb') for _ in range(2)]
        pout = [a.ps([2, 512], F32, 'pout') for _ in range(2)]
        ST3 = ST[:, :].rearrange("p (h v) -> p h v", v=64)

        def b16(buf, q, s):
            return buf[:, q, :, s:s + 1].to_broadcast([128, 16, 64])
        for i in range(T):
            blk, s = i // BSZ, i % BSZ
            sdb = sd[blk % 2]
            if s == 0:
                for d, SD in enumerate((self.SDf, self.SDb)):
                    P.dma('sp', sdb[d * 64:(d + 1) * 64, :, :, :], bass.AP(SD, blk * BSZ, [[5 * 16 * T, 64], [T, 80], [1, BSZ]]),
                          writes=[sdb])
            vi, vs_ = i // VS, i % VS
            vrb, ob = vr[vi % 2], ost[vi % 2]
            if vs_ == 0:
                P.dma('sp', vrb[0:1, :, :], self.Vf[i:i + VS, :].rearrange("(o s) n -> o s n", o=1), writes=[vrb])
                P.dma('sp', vrb[1:2, :, :], self.Vb[i:i + VS, :].rearrange("(o s) n -> o s n", o=1), writes=[vrb])
            tm, vb_, t3_, t4_ = tmp[i % 2], vbs[i % 2], t3[i % 2], t4[i % 2]
            tm3 = tm[:, :].rearrange("p (h v) -> p h v", v=64)
            for hf in range(2):
                P.mm(pvb[hf][:, :], SEL[:, :], vrb[:, vs_, hf * 512:(hf + 1) * 512], True, True, [SEL, vrb], [pvb[hf]])
                P.act(vb_[:, hf * 512:(hf + 1) * 512], pvb[hf][:, :], AF.Copy, [pvb[hf]], [vb_])
            P.op('pool', lambda h, t3_=t3_, vb_=vb_, sdb=sdb, s=s: h.tensor_tensor(
                t3_[:, :].rearrange("p (h v) -> p h v", v=64), vb_[:, :].rearrange("p (h v) -> p h v", v=64), b16(sdb, 2, s), ALU.mult),
                [vb_, sdb], [t3_])
            P.op('dve', lambda h, tm3=tm3, sdb=sdb, s=s: h.tensor_tensor(tm3, ST3, b16(sdb, 0, s), ALU.mult), [ST, sdb], [tm])
            for hf in range(2):
                P.mm(pskk[hf][:, :], BO[:, :], tm[:, hf * 512:(hf + 1) * 512], True, True, [BO, tm], [pskk[hf]])
            P.op('dve', lambda h, sdb=sdb, s=s: h.tensor_tensor(ST3, ST3, b16(sdb, 3, s), ALU.mult), [ST, sdb], [ST])
            for hf in range(2):
                P.op('dve', lambda h, hf=hf, sdb=sdb, s=s: h.tensor_tensor(
                    t2[:, hf * 512:(hf + 1) * 512].rearrange("p (h v) -> p h v", v=64),
                    pskk[hf][:, :].rearrange("p (h v) -> p h v", v=64),
                    sdb[:, 1, hf * 8:(hf + 1) * 8, s:s + 1].to_broadcast([128, 8, 64]), ALU.mult), [pskk[hf], sdb], [t2])
            P.op('dve', lambda h: h.tensor_tensor(ST[:, :], ST[:, :], t2[:, :], ALU.subtract), [ST, t2], [ST])
            P.op('dve', lambda h, t3_=t3_: h.tensor_tensor(ST[:, :], ST[:, :], t3_[:, :], ALU.add), [ST, t3_], [ST])
            P.op('pool', lambda h, t4_=t4_, sdb=sdb, s=s: h.tensor_tensor(
                t4_[:, :].rearrange("p (h v) -> p h v", v=64), ST3, b16(sdb, 4, s), ALU.mult), [ST, sdb], [t4_])
            for hf in range(2):
                P.mm(pout[hf][:, :], SEL2[:, :], t4_[:, hf * 512:(hf + 1) * 512], True, True, [SEL2, t4_], [pout[hf]])
                P.act(ob[:, vs_, hf * 512:(hf + 1) * 512], pout[hf][:, :], AF.Copy, [pout[hf]], [ob])
            if vs_ == VS - 1:
                i0 = i - VS + 1
                P.dma('sp', self.wkvf[i0:i0 + VS, :].rearrange("(o s) n -> o s n", o=1), ob[0:1, :, :], reads=[ob])
                P.dma('sp', self.wkvb[i0:i0 + VS, :].rearrange("(o s) n -> o s n", o=1), ob[1:2, :, :], reads=[ob])
        P.barrier()
        P.flush()
        a.close()

    def rwkv_post(self, l):
        P, nc, c = self.P, self.nc, self.c
        self.dbg_dump(l, 'wkvf', self.wkvf[:, :])
        self.dbg_dump(l, 'SDf', self.SDf[:, :])
        self.dbg_dump(l, 'SDb', self.SDb[:, :])
        self.dbg_dump(l, 'Vf', self.Vf[:, :])
        self.dbg_dump(l, 'Vb', self.Vb[:, :])
        self.dbg_dump(l, 'wkvb', self.wkvb[:, :])
        NT, T = c['NT'], c['T']
        a = Alloc(nc)
        gnw = a.sb([128, 1024], F32, 'gnw')
        gnb = a.sb([128, 1024], F32, 'gnb')
        P.dma('sp', gnw[:, :], self.svec(l, 'rw_gn_w'), writes=[gnw])
        P.dma('sp', gnb[:, :], self.svec(l, 'rw_gn_b'), writes=[gnb])
        jf = a.sb([128, 128], F32, 'jfull')
        P.dma('sp', jf[:, :], self.consts[:, 320:448], writes=[jf])
        wf = [a.sb([128, 1024], F32, 'wf') for _ in range(2)]
        wb = [a.sb([128, 1024], F32, 'wb') for _ in range(2)]
        vv = [a.sb([128, 1024], F32, 'vv') for _ in range(2)]
        gg = [a.sb([128, 1024], F32, 'gg') for _ in range(2)]
        bs = [a.sb([128, 16], F32, 'bs') for _ in range(2)]
        sq = a.sb([128, 1024], F32, 'sq')
        sm = a.sb([128, 64], F32, 'sm')
        pm = [a.ps([128, 512], F32, 'pm') for _ in range(2)]
        for tt in range(self.t_lo, NT):
            i = tt % 2
            s0f, s0b = self.step_off(tt)
            P.dma('sp', wf[i][:, :], self.wkvf[s0f:s0f + 128, :], writes=[wf[i]])
            P.dma('sp', wb[i][:, :], self.wkvb[s0b:s0b + 128, :], writes=[wb[i]])
            P.dma('sp', vv[i][:, :], self.Vf[tt * 128:(tt + 1) * 128, :], writes=[vv[i]])
            P.dma('sp', gg[i][:, :], self.Gd[tt * 128:(tt + 1) * 128, :], writes=[gg[i]])
            P.dma('sp', bs[i][:, :], self.BSd[tt * 128:(tt + 1) * 128, :], writes=[bs[i]])
            for hf in range(2):
                P.mm(pm[hf][:, :], jf[:, :], wb[i][:, hf * 512:(hf + 1) * 512], True, True, [jf, wb[i]], [pm[hf]])
                P.op('dve', lambda h, hf=hf, i=i: h.tensor_tensor(wf[i][:, hf * 512:(hf + 1) * 512], wf[i][:, hf * 512:(hf + 1) * 512],
                                                                pm[hf][:, :], ALU.add), [wf[i], pm[hf]], [wf[i]])
            w3 = wf[i][:, :].rearrange("p (h k) -> p h k", k=64)
            s3 = sq[:, :].rearrange("p (h k) -> p h k", k=64)
            P.op('dve', lambda h, w3=w3: h.reduce_sum(sm[:, 0:16], w3, AX.X), [wf[i]], [sm])
            P.op('dve', lambda h: h.tensor_scalar(sm[:, 0:16], sm[:, 0:16], 1.0 / 64, None, ALU.mult), [sm], [sm])
            P.op('dve', lambda h, w3=w3: h.tensor_tensor(w3, w3, sm[:, 0:16].unsqueeze(2).to_broadcast([128, 16, 64]), ALU.subtract),
                 [wf[i], sm], [wf[i]])
            P.op('dve', lambda h, i=i: h.tensor_tensor(sq[:, :], wf[i][:, :], wf[i][:, :], ALU.mult), [wf[i]], [sq])
            P.op('dve', lambda h, s3=s3: h.reduce_sum(sm[:, 16:32], s3, AX.X), [sq], [sm])
            P.op('dve', lambda h: h.tensor_scalar(sm[:, 16:32], sm[:, 16:32], 1.0 / 64, 64e-5, ALU.mult, ALU.add), [sm], [sm])
            P.act(sm[:, 16:32], sm[:, 16:32], AF.Sqrt, [sm], [sm])
            P.op('dve', lambda h: h.reciprocal(sm[:, 32:48], sm[:, 16:32]), [sm], [sm])
            P.op('dve', lambda h, w3=w3: h.tensor_tensor(w3, w3, sm[:, 32:48].unsqueeze(2).to_broadcast([128, 16, 64]), ALU.mult),
                 [wf[i], sm], [wf[i]])
            P.op('dve', lambda h, i=i: h.tensor_tensor(wf[i][:, :], wf[i][:, :], gnw[:, :], ALU.mult), [wf[i], gnw], [wf[i]])
            P.op('dve', lambda h, i=i: h.tensor_tensor(wf[i][:, :], wf[i][:, :], gnb[:, :], ALU.add), [wf[i], gnb], [wf[i]])
            v3 = vv[i][:, :].rearrange("p (h k) -> p h k", k=64)
            P.op('dve', lambda h, v3=v3, i=i: h.tensor_tensor(v3, v3, bs[i][:, :].unsqueeze(2).to_broadcast([128, 16, 64]), ALU.mult),
                 [vv[i], bs[i]], [vv[i]])
            P.op('dve', lambda h, i=i: h.tensor_tensor(wf[i][:, :], wf[i][:, :], vv[i][:, :], ALU.add), [wf[i], vv[i]], [wf[i]])
            P.op('dve', lambda h, i=i: h.tensor_tensor(wf[i][:, :], wf[i][:, :], gg[i][:, :], ALU.mult), [wf[i], gg[i]], [wf[i]])
            P.dma('sp', self.y_rw[tt * 128:(tt + 1) * 128, :], wf[i][:, :], reads=[wf[i]])
        self.dbg_dump(l, 'y_rw', self.y_rw[:, :])
        P.barrier()
        P.flush()
        a.close()

    def transpose_bf(self, src_bf, dstT, nk, pst, cnt):
        P = self.P
        for g in range(0, nk, 8):
            ng = min(8, nk - g)
            p = pst[cnt[0] % len(pst)]
            cnt[0] += 1
            for i in range(ng):
                kt = g + i
                P.tr(p[:, i, :], src_bf[:, kt * 128:(kt + 1) * 128], self.ident_b[:, :], [src_bf, self.ident_b], [p])
            if cnt[0] % 2:
                P.op('act', lambda h, p=p, g=g, ng=ng: h.copy(dstT[:, g:g + ng, :], p[:, 0:ng, :]), [p], [dstT])
            else:
                P.op('dve', lambda h, p=p, g=g, ng=ng: h.tensor_copy(dstT[:, g:g + ng, :], p[:, 0:ng, :]), [p], [dstT])

    def phase_merge(self, l):
        P, nc, c = self.P, self.nc, self.c
        D, NT = c['D'], c['NT']
        NCH = D // 512
        o_gate = c['RW_COLS'] + 6144
        a = Alloc(nc)
        wo = []
        for name in ('w_out_rw', 'w_out_na', 'w_out_sc'):
            w = a.sb([128, 8, D], BF16, name)
            self.load_w('sp', w, l, name, 8, 0, D, bf=True)
            wo.append(w)
        bgt = a.sb([128, 3 * D], F32, 'bgate')
        P.dma('sp', bgt[:, :], self.svec(l, 'b_gate'), writes=[bgt])
        ysrc = (self.y_rw, self.y_na, self.y_sc)
        yf = [[a.sb([128, 1024], F32, 'yf') for _ in range(3)] for _ in range(2)]
        yb = [[a.sb([128, 1024], BF16, 'yb') for _ in range(3)] for _ in range(2)]
        yT = [[a.sb([128, 8, 128], BF16, 'yT') for _ in range(3)] for _ in range(2)]
        pst = [a.ps([128, 8, 128], BF16, 'pst') for _ in range(2)]
        pso = [a.ps([128, 512], F32, 'pso') for _ in range(3)]
        gt = [a.sb([128, 3, 512], F32, 'gt') for _ in range(2)]
        mt = [a.sb([128, D], F32, 'mt') for _ in range(2)]
        cnt = [0]
        n = 0
        for tt in range(self.t_lo, NT):
            m = mt[tt % 2]
            for br in range(3):
                f, b_, t_ = yf[tt % 2][br], yb[tt % 2][br], yT[tt % 2][br]
                P.dma('sp', f[:, :], ysrc[br][tt * 128:(tt + 1) * 128, :], writes=[f])
                P.op('pool', lambda h, f=f, b_=b_: h.tensor_copy(b_[:, :], f[:, :]), [f], [b_])
                self.transpose_bf(b_, t_, 8, pst, cnt)
            for j in range(NCH):
                g = gt[n % 2]
                n += 1
                P.dma('sp', g[:, :, :], bass.AP(self.proj.tensor, tt * 128 * c['IN_COLS'] + o_gate + j * 512,
                                                [[c['IN_COLS'], 128], [D, 3], [1, 512]]), writes=[g])
                for br in range(3):
                    P.op('dve', lambda h, g=g, br=br, j=j: h.tensor_tensor(g[:, br, :], g[:, br, :],
                                                                             bgt[:, br * D + j * 512:br * D + (j + 1) * 512], ALU.add),
                         [g, bgt], [g])
                P.act(g[:, :, :], g[:, :, :], AF.Sigmoid, [g], [g])
                for br in range(3):
                    t_ = yT[tt % 2][br]
                    p = pso[br]
                    for kt in range(8):
                        P.mm(p[:, :], t_[:, kt, :], wo[br][:, kt, j * 512:(j + 1) * 512], kt == 0, kt == 7, [t_, wo[br]], [p])
                P.op('dve', lambda h, g=g, j=j, m=m: h.tensor_tensor(m[:, j * 512:(j + 1) * 512], pso[0][:, :], g[:, 0, :], ALU.mult),
                     [pso[0], g], [m])
                for br in (1, 2):
                    P.op('dve', lambda h, g=g, br=br: h.tensor_tensor(g[:, br, :], pso[br][:, :], g[:, br, :], ALU.mult), [pso[br], g], [g])
                    P.op('pool', lambda h, g=g, br=br, j=j, m=m: h.tensor_tensor(m[:, j * 512:(j + 1) * 512], m[:, j * 512:(j + 1) * 512],
                                                                                 g[:, br, :], ALU.add), [m, g], [m])
            P.dma('sp', self.mmix[tt * 128:(tt + 1) * 128, :], m[:, :], reads=[m])
        P.barrier()
        P.flush()
        a.close()

    def phase_post1(self, l):
        P, nc, c = self.P, self.nc, self.c
        D, NT = c['D'], c['NT']
        KT = D // 128
        NCH = D // 512
        LT = c['L'] // 128
        a = Alloc(nc)
        wm = a.sb([128, KT, D], BF16, 'wmerge')
        self.load_w('sp', wm, l, 'w_merge', KT, 0, D, bf=True)
        lng = a.sb([128, D], F32, 'ln1g')
        lnb = a.sb([128, D], F32, 'ln1b')
        P.dma('sp', lng[:, :], self.svec(l, 'ln1_g'), writes=[lng])
        P.dma('sp', lnb[:, :], self.svec(l, 'ln1_b'), writes=[lnb])
        mods = {}
        for row in ((0, 1) if self.t_lo == 0 else (1,)):
            g1 = self.mod_bcast(a, row, 2, 'g1')
            sc2 = self.mod_bcast(a, row, 4, 'sc2')
            sh2 = self.mod_bcast(a, row, 3, 'sh2')
            P.op('dve', lambda h, sc2=sc2: h.tensor_scalar_add(sc2[:, :], sc2[:, :], 1.0), [sc2], [sc2])
            mods[row] = (g1, sc2, sh2)
        mf = [a.sb([128, D], F32, 'mf')] * 2
        mb = [a.sb([128, D], BF16, 'mb') for _ in range(2)]
        mT = [a.sb([128, KT, 128], BF16, 'mT') for _ in range(2)]
        xt = [a.sb([128, D], F32, 'xt')] * 2
        x1 = [a.sb([128, D], F32, 'x1')] * 2
        hh = [a.sb([128, D], F32, 'hh')] * 2
        tmp = dict(st=a.sb([128, 8], F32, 'st'), junk=a.sb([128, D], F32, 'junk'), junk2=a.sb([128, D], F32, 'junk2'))
        pst = [a.ps([128, 8, 128], BF16, 'pst') for _ in range(2)]
        pso = [a.ps([128, 512], F32, 'pso') for _ in range(2)]
        cnt = [0]
        n = 0
        for tt in range(self.t_lo, NT):
            i = tt % 2
            g1, sc2, sh2 = mods[0 if tt < LT else 1]
            P.dma('sp', mf[i][:, :], self.mmix[tt * 128:(tt + 1) * 128, :], writes=[mf[i]])
            P.dma('sp', xt[i][:, :], self.xres[tt * 128:(tt + 1) * 128, :], writes=[xt[i]])
            P.op('pool', lambda h, i=i: h.tensor_copy(mb[i][:, :], mf[i][:, :]), [mf[i]], [mb[i]])
            self.transpose_bf(mb[i], mT[i], KT, pst, cnt)
            for j in range(NCH):
                p = pso[n % 2]
                n += 1
                for kt in range(KT):
                    P.mm(p[:, :], mT[i][:, kt, :], wm[:, kt, j * 512:(j + 1) * 512], kt == 0, kt == KT - 1, [mT[i], wm], [p])
                P.op('dve', lambda h, p=p, j=j, i=i, g1=g1: h.tensor_tensor(mf[i][:, j * 512:(j + 1) * 512], p[:, :],
                                                                          g1[:, j * 512:(j + 1) * 512], ALU.mult), [p, g1, mb[i]], [mf[i]])
            P.op('dve', lambda h, i=i: h.scalar_tensor_tensor(xt[i][:, :], xt[i][:, :], float(c['alpha']), mf[i][:, :], ALU.mult, ALU.add),
                 [xt[i], mf[i]], [xt[i]])
            self.layernorm_tile(a, xt[i], x1[i], lng, lnb, tmp)
            P.dma('sp', self.xres[tt * 128:(tt + 1) * 128, :], x1[i][:, :], reads=[x1[i]])
            self.layernorm_tile(a, x1[i], hh[i], sc2, sh2, tmp)
            P.dma('sp', self.h2[tt * 128:(tt + 1) * 128, :], hh[i][:, :], reads=[hh[i]])
        self.dbg_dump(l, 'x1', self.xres[:, :])
        self.dbg_dump(l, 'h2', self.h2[:, :])
        P.barrier()
        P.flush()
        a.close()

    def phase_moe(self, l):
        P, nc, c = self.P, self.nc, self.c
        D, NT, NE = c['D'], c['NT'], c['NE']
        KT = D // 128
        NCH = D // 512
        LT = c['L'] // 128
        GS = c.get('MOE_GROUP', 4)
        a = Alloc(nc)
        rw = a.sb([128, KT, NE], F32, 'rw')
        P.dma('sp', rw[:, :, :], bass.AP(self.small.tensor, l * self.NS + self.soff['router_w'], [[NE, 128], [128 * NE, KT], [1, NE]]),
              writes=[rw])
        rb = a.sb([128, NE], F32, 'rb')
        P.dma('sp', rb[:, :], self.svec(l, 'router_b'), writes=[rb])
        lng = a.sb([128, D], F32, 'ln2g')
        lnb = a.sb([128, D], F32, 'ln2b')
        P.dma('sp', lng[:, :], self.svec(l, 'ln2_g'), writes=[lng])
        P.dma('sp', lnb[:, :], self.svec(l, 'ln2_b'), writes=[lnb])
        g2s = {row: self.mod_bcast(a, row, 5, 'g2') for row in ((0, 1) if self.t_lo == 0 else (1,))}
        hT = a.sb([128, GS, KT, 128], BF16, 'hT')
        gate = a.sb([128, GS, NE], F32, 'gate')
        yacc = a.sb([128, GS, D], F32, 'yacc')
        hf = [a.sb([128, D], F32, 'hf')] * 2
        hb = a.sb([128, D], BF16, 'hb')
        hTf = a.sb([128, KT, 128], F32, 'hTf')
        w1 = [a.sb([128, KT, 1024], BF16, 'w1')] * 2
        w2 = [a.sb([128, 4, D], BF16, 'w2') for _ in range(2)]
        b1 = [a.sb([128, 1024], F32, 'b1')] * 2
        b2 = [a.sb([128, D], F32, 'b2')] * 2
        zt = a.sb([128, 1024], F32, 'zt')
        sg = a.sb([128, 512], F32, 'sg')
        ab = a.sb([128, 512], BF16, 'ab')
        aT = a.sb([128, 4, 128], BF16, 'aT')
        yt = a.sb([128, 512], F32, 'yt')
        st = a.sb([128, 16], F32, 'stm')
        lg = a.sb([128, NE], F32, 'lg')
        tmp = dict(st=a.sb([128, 8], F32, 'st'), junk=a.sb([128, D], F32, 'junk'), junk2=a.sb([128, D], F32, 'junk2'))
        pst = [a.ps([128, 8, 128], BF16, 'pst')]
        psf = a.ps([128, 4, 128], F32, 'psf')
        psz = [a.ps([128, 512], F32, 'psz') for _ in range(2)]
        psy = [a.ps([128, 512], F32, 'psy') for _ in range(2)]
        psl = a.ps([128, NE], F32, 'psl')
        cnt = [0]
        ny = 0
        tiles = list(range(self.t_lo, NT))
        for g0 in range(0, len(tiles), GS):
            grp = tiles[g0:g0 + GS]
            for gi, tt in enumerate(grp):
                f = hf[gi % 2]
                P.dma('sp', f[:, :], self.h2[tt * 128:(tt + 1) * 128, :], writes=[f])
                P.op('pool', lambda h, f=f: h.tensor_copy(hb[:, :], f[:, :]), [f], [hb])
                self.transpose_bf(hb, _View(hT, gi), KT, pst, cnt)
                for q in range(0, KT, 4):
                    for i in range(4):
                        P.tr(psf[:, i, :], f[:, (q + i) * 128:(q + i + 1) * 128], self.ident_f[:, :], [f, self.ident_f], [psf])
                    P.op('dve', lambda h, q=q: h.tensor_copy(hTf[:, q:q + 4, :], psf[:, :, :]), [psf], [hTf])
                for kt in range(KT):
                    P.mm(psl[:, :], hTf[:, kt, :], rw[:, kt, :], kt == 0, kt == KT - 1, [hTf, rw], [psl])
                P.op('dve', lambda h: h.tensor_tensor(lg[:, :], psl[:, :], rb[:, :], ALU.add), [psl, rb], [lg])
                self.dbg_gate_logits = lg
                P.op('dve', lambda h: h.max(st[:, 0:8], lg[:, :]), [lg], [st])
                P.op('dve', lambda h, gi=gi: h.tensor_scalar(gate[:, gi, :], lg[:, :], st[:, 3:4], None, ALU.is_ge), [lg, st], [gate])
                P.op('dve', lambda h: h.tensor_scalar(st[:, 8:9], st[:, 0:1], -1.0, None, ALU.mult), [st], [st])
                P.act(lg[:, :], lg[:, :], AF.Exp, [lg, st], [lg], bias=st[:, 8:9])
                P.op('dve', lambda h, gi=gi: h.tensor_tensor(gate[:, gi, :], gate[:, gi, :], lg[:, :], ALU.mult), [gate, lg], [gate])
                P.op('dve', lambda h, gi=gi: h.reduce_sum(st[:, 9:10], gate[:, gi, :], AX.X), [gate], [st])
                P.op('dve', lambda h: h.reciprocal(st[:, 10:11], st[:, 9:10]), [st], [st])
                P.op('dve', lambda h, gi=gi: h.tensor_scalar(gate[:, gi, :], gate[:, gi, :], st[:, 10:11], None, ALU.mult), [gate, st], [gate])
                if 'gate' in self.dbg_out and l == self.dbg.get('_layer', 0):
                    P.dma('sp', self.dbg_out['gate'][tt * 128:(tt + 1) * 128, :], gate[:, gi, :], reads=[gate])
            P.op('pool', lambda h: h.memset(yacc[:, :, :], 0.0), [], [yacc])
            for e in range(NE):
                w1e, w2e, b1e, b2e = w1[e % 2], w2[e % 2], b1[e % 2], b2[e % 2]
                self.load_w('sp', w1e, l, 'moe_w1', KT, 0, 1024, bf=True, k0=e * KT)
                self.load_w('sp', w2e, l, 'moe_w2', 4, 0, D, bf=True, k0=e * 4)
                P.dma('sp', b1e[:, :], self.svec(l, 'moe_b1', o=e * 1024, n=1024), writes=[b1e])
                P.dma('sp', b2e[:, :], self.svec(l, 'moe_b2', o=e * D, n=D), writes=[b2e])
                for gi, tt in enumerate(grp):
                    for hf_ in range(2):
                        p = psz[hf_]
                        for kt in range(KT):
                            P.mm(p[:, :], hT[:, gi, kt, :], w1e[:, kt, hf_ * 512:(hf_ + 1) * 512], kt == 0, kt == KT - 1, [hT, w1e], [p])
                        P.op('dve', lambda h, p=p, hf_=hf_, b1e=b1e: h.tensor_tensor(zt[:, hf_ * 512:(hf_ + 1) * 512], p[:, :],
                                                                                  b1e[:, hf_ * 512:(hf_ + 1) * 512], ALU.add), [p, b1e], [zt])
                    zv = zt[:, :].rearrange("p (f two) -> p f two", two=2)
                    P.op('dve', lambda h, zv=zv: h.tensor_scalar(zv[:, :, 0], zv[:, :, 0], 7.0, None, ALU.min), [zt], [zt])
                    P.op('pool', lambda h, zv=zv: h.tensor_scalar(zv[:, :, 1], zv[:, :, 1], 7.0, -7.0, ALU.min, ALU.max), [zt], [zt])
                    P.act(sg[:, :], zv[:, :, 0], AF.Sigmoid, [zt], [sg], scale=1.702)
                    P.op('pool', lambda h, zv=zv: h.tensor_scalar(zv[:, :, 1], zv[:, :, 1], 1.0, None, ALU.add), [zt], [zt])
                    P.op('dve', lambda h, zv=zv: h.tensor_tensor(sg[:, :], sg[:, :], zv[:, :, 0], ALU.mult), [sg, zt], [sg])
                    P.op('dve', lambda h, zv=zv: h.tensor_tensor(ab[:, :], sg[:, :], zv[:, :, 1], ALU.mult), [sg, zt], [ab])
                    self.transpose_bf(ab, aT, 4, pst, cnt)
                    for j in range(NCH):
                        p = psy[ny % 2]
                        ny += 1
                        for kt in range(4):
                            P.mm(p[:, :], aT[:, kt, :], w2e[:, kt, j * 512:(j + 1) * 512], kt == 0, kt == 3, [aT, w2e], [p])
                        P.op('dve', lambda h, p=p, j=j, b2e=b2e: h.tensor_tensor(yt[:, :], p[:, :], b2e[:, j * 512:(j + 1) * 512], ALU.add),
                             [p, b2e], [yt])
                        P.op('dve', lambda h, gi=gi, j=j, e=e: h.scalar_tensor_tensor(
                            yacc[:, gi, j * 512:(j + 1) * 512], yt[:, :], gate[:, gi, e:e + 1], yacc[:, gi, j * 512:(j + 1) * 512],
                            ALU.mult, ALU.add), [yt, gate, yacc], [yacc])
            for gi, tt in enumerate(grp):
                f = hf[gi % 2]
                g2 = g2s[0 if tt < LT else 1]
                P.dma('sp', f[:, :], self.xres[tt * 128:(tt + 1) * 128, :], writes=[f])
                if 'moe' in self.dbg_out and l == self.dbg.get('_layer', 0):
                    P.dma('sp', self.dbg_out['moe'][tt * 128:(tt + 1) * 128, :], yacc[:, gi, :], reads=[yacc])
                P.op('dve', lambda h, gi=gi, g2=g2: h.tensor_tensor(yacc[:, gi, :], yacc[:, gi, :], g2[:, :], ALU.mult), [yacc, g2], [yacc])
                P.op('dve', lambda h, gi=gi, f=f: h.scalar_tensor_tensor(f[:, :], f[:, :], float(c['alpha']), yacc[:, gi, :], ALU.mult, ALU.add),
                     [f, yacc], [f])
                self.layernorm_tile(a, f, hTf_out(hf, gi, f), lng, lnb, tmp)
                P.dma('sp', self.xres[tt * 128:(tt + 1) * 128, :], f[:, :], reads=[f])
        self.dbg_dump(l, 'x2', self.xres[:, :])
        P.barrier()
        P.flush()
        a.close()

    def phase_mod(self, l):
        P, nc, c = self.P, self.nc, self.c
        D = c['D']
        KT = D // 128
        a = Alloc(nc)
        cs = a.sb([128, KT, 2], F32, 'cs')
        for r in range(2):
            P.dma('sp', cs[:, :, r:r + 1], bass.AP(self.cvec.tensor, (self.bi * 2 + r) * D, [[1, 128], [128, KT], [1, 1]]), writes=[cs],
                  allow_slow_non_contiguous=True)
        P.act(cs[:, :, :], cs[:, :, :], AF.Silu, [cs], [cs])
        NCH = 6 * D // 512
        wt = [a.sb([128, KT, 512], F32, 'wmod') for _ in range(2)]
        bm = a.sb([2, 6 * D], F32, 'bmod')
        P.dma('sp', bm[:, :], self.svec(l, 'b_mod', parts=2), writes=[bm])
        ps = [a.ps([2, 512], F32, 'psmod') for _ in range(2)]
        res = a.sb([2, 6 * D], F32, 'modres')
        for j in range(NCH):
            w = wt[j % 2]
            self.load_w('sp', w, l, 'w_mod', KT, j * 512, 512, bf=False)
            p = ps[j % 2]
            for kt in range(KT):
                P.mm(p[:, :], cs[:, kt, :], w[:, kt, :], kt == 0, kt == KT - 1, [cs, w], [p])
            P.op('dve', lambda h, p=p, j=j: h.tensor_tensor(res[:, j * 512:(j + 1) * 512], p[:, :],
                                                            bm[:, j * 512:(j + 1) * 512], ALU.add), [p, bm], [res])
        P.dma('sp', self.modd[:, :], res[:, :], reads=[res])
        if 'modd' in self.dbg_out and l == self.dbg.get('_layer', 0):
            P.dma('sp', self.dbg_out['modd'][:, :], res[:, :], reads=[res])
        P.barrier()
        P.flush()
        a.close()

    def mod_bcast(self, a, row, idx, name):
        D = self.c['D']
        t = a.sb([128, D], F32, name)
        src = bass.AP(self.modd.tensor, row * 6 * D + idx * D, [[0, 128], [1, D]])
        self.P.dma('sp', t[:, :], src, writes=[t])
        return t

    def layernorm_tile(self, a, xt, out, scale_t, shift_t, tmp):
        P, c = self.P, self.c
        D = c['D']
        st, junk = tmp['st'], tmp['junk']
        P.act(junk[:, :], xt[:, :], AF.Identity, [xt], [junk, st], accum_out=st[:, 0:1])
        P.op('dve', lambda h: h.tensor_scalar(st[:, 1:2], st[:, 0:1], -1.0 / D, None, ALU.mult), [st], [st])
        P.op('dve', lambda h: h.tensor_scalar(junk[:, :], xt[:, :], st[:, 1:2], None, ALU.add), [xt, st], [junk])
        P.act(tmp['junk2'][:, :], junk[:, :], AF.Square, [junk], [tmp['junk2'], st], accum_out=st[:, 2:3])
        P.op('dve', lambda h: h.tensor_scalar(st[:, 3:4], st[:, 2:3], 1.0 / D, 1e-6, ALU.mult, ALU.add), [st], [st])
        P.act(st[:, 4:5], st[:, 3:4], AF.Sqrt, [st], [st])
        P.op('dve', lambda h: h.reciprocal(st[:, 3:4], st[:, 4:5]), [st], [st])
        P.op('dve', lambda h: h.scalar_tensor_tensor(junk[:, :], junk[:, :], st[:, 3:4], scale_t[:, :], ALU.mult, ALU.mult),
             [junk, st, scale_t], [junk])
        P.op('dve', lambda h: h.tensor_tensor(out[:, :], junk[:, :], shift_t[:, :], ALU.add), [junk, shift_t], [out])

    def phase_inproj(self, l):
        P, nc, c = self.P, self.nc, self.c
        D, NT, T = c['D'], c['NT'], c['T']
        KT = D // 128
        LT = c['L'] // 128
        a = Alloc(nc)
        hT = a.sb([128, NT, KT, 128], BF16, 'hT')
        mods = {}
        for row in (0, 1):
            sc = self.mod_bcast(a, row, 1, 'sc1')
            sh = self.mod_bcast(a, row, 0, 'sh1')
            P.op('dve', lambda h, sc=sc: h.tensor_scalar_add(sc[:, :], sc[:, :], 1.0), [sc], [sc])
            mods[row] = (sc, sh)
        xts = [a.sb([128, D], F32, 'xt') for _ in range(2)]
        tmp = dict(st=a.sb([128, 8], F32, 'st'), junk=a.sb([128, D], F32, 'junk'), junk2=a.sb([128, D], F32, 'junk2'))
        hb = [a.sb([128, D], BF16, 'hb') for _ in range(2)]
        pst = [a.ps([128, 8, 128], BF16, 'pst') for _ in range(2)]
        ntr = 0
        for tt in range(NT):
            xt = xts[tt % 2]
            P.dma('sp', xt[:, :], self.xres[tt * 128:(tt + 1) * 128, :], writes=[xt])
            sc, sh = mods[0 if tt < LT else 1]
            h_ = hb[tt % 2]
            self.layernorm_tile(a, xt, h_, sc, sh, tmp)
            for g in range(KT // 8 if KT >= 8 else 1):
                ng = min(8, KT)
                p = pst[ntr % 2]
                ntr += 1
                for i in range(ng):
                    kt = g * 8 + i
                    P.tr(p[:, i, :], h_[:, kt * 128:(kt + 1) * 128], self.ident_b[:, :], [h_, self.ident_b], [p])
                P.op('act' if ntr % 2 else 'dve',
                     (lambda h, p=p, tt=tt, g=g, ng=ng: h.copy(hT[:, tt, g * 8:g * 8 + ng, :], p[:, 0:ng, :])) if ntr % 2 else
                     (lambda h, p=p, tt=tt, g=g, ng=ng: h.tensor_copy(hT[:, tt, g * 8:g * 8 + ng, :], p[:, 0:ng, :])),
                     [p], [hT])
        NCOL = c['IN_COLS']
        wts = [a.sb([128, KT, 512], BF16, 'win') for _ in range(2)]
        pso = [a.ps([128, 512], F32, 'pso') for _ in range(4)]
        osb = [a.sb([128, 512], F32, 'osb') for _ in range(4)]
        j = 0
        n = 0
        c0 = 0
        while c0 < NCOL:
            w_ = min(512, NCOL - c0)
            w = wts[j % 2]
            self.load_w('sp', w, l, 'w_in', KT, c0, w_, bf=True)
            for tt in range(NT):
                p = pso[n % 4]
                o = osb[n % 4]
                for kt in range(KT):
                    P.mm(p[:, 0:w_], hT[:, tt, kt, :], w[:, kt, 0:w_], kt == 0, kt == KT - 1, [hT, w], [p])
                if n % 2:
                    P.op('act', lambda h, p=p, o=o, w_=w_: h.copy(o[:, 0:w_], p[:, 0:w_]), [p], [o])
                else:
                    P.op('dve', lambda h, p=p, o=o, w_=w_: h.tensor_copy(o[:, 0:w_], p[:, 0:w_]), [p], [o])
                P.dma('sp', self.proj[tt * 128:(tt + 1) * 128, c0:c0 + w_], o[:, 0:w_], reads=[o])
                n += 1
            c0 += w_
            j += 1
        P.barrier()
        if 'proj' in self.dbg_out and l == self.dbg.get('_layer', 0):
            shp = self.dbg['proj']
            P.dma('sp', self.dbg_out['proj'][:, :], self.proj[0:shp[0], 0:shp[1]])
            P.barrier()
        P.flush()
        a.close()

    def finish(self):
        self.P.barrier()
        self.P.flush()
        self.cst.close()


def host_consts():
    cst = np.zeros((128, 1024), np.float32)
    cst[:, 0:128] = np.eye(128, dtype=np.float32)
    rev = np.zeros((128, 128), np.float32)
    for i in range(128):
        rev[i, (i // 64) * 64 + 63 - (i % 64)] = 1.0
    cst[:, 128:256] = rev
    qc = np.arange(64)
    ws = np.clip(qc - 8, 0, 48)
    kc = np.arange(64)
    inside = (kc[None, :] >= ws[:, None]) & (kc[None, :] < ws[:, None] + 16)
    cst[0:64, 256:320] = np.where(inside, 0.0, -1e30).astype(np.float32)
    full = np.zeros((128, 128), np.float32)
    for i in range(128):
        full[i, 127 - i] = 1.0
    cst[:, 320:448] = full
    bo = np.zeros((128, 128), np.float32)
    bo[0:64, 0:64] = 1.0
    bo[64:128, 64:128] = 1.0
    cst[:, 448:576] = bo
    cst[0, 576:640] = 1.0
    cst[1, 640:704] = 1.0
    cst[0:64, 704] = 1.0
    cst[64:128, 705] = 1.0
    return cst


def host_rope(S):
    t = np.arange(S)
    inv = np.power(np.float32(10000.0), -np.arange(16, dtype=np.float32) / 16).astype(np.float32)
    row = (t // 64).astype(np.float32)
    col = (t % 64).astype(np.float32)
    ang = np.concatenate([row[:, None] * inv, col[:, None] * inv], -1).astype(np.float32)
    return np.concatenate([np.cos(ang), np.sin(ang)], -1).astype(np.float32)


def host_inputs(c, inp, ncores, NB):
    DEPTH, D = c['DEPTH'], c['D']
    soff, ssz, NS = small_layout(c)
    small = np.empty((DEPTH, NS), np.float32)
    for k in SMALL:
        small[:, soff[k]:soff[k] + ssz[k]] = np.asarray(inp[k], np.float32).reshape(DEPTH, -1)
    rpb = np.asarray(inp['na_rpb'], np.float32)
    rpbp = np.zeros((DEPTH, 16, 15, 127), np.float32)
    rpbp[..., 48:79] = rpb[..., ::-1]
    rpbp = rpbp.reshape(DEPTH * 16 * 15, 127)
    cst = host_consts()
    rope = host_rope(c['S'])
    wts = {}
    for k, (r, w) in big_shapes(c).items():
        wts['W_' + k] = np.ascontiguousarray(np.asarray(inp[k], np.float32).reshape(DEPTH * r, w))
    B = inp['x'].shape[0]
    maps = []
    for r in range(ncores):
        bs = [(r * NB + i) % B for i in range(NB)]
        cv = np.concatenate([np.stack([np.asarray(inp['c_ctx'], np.float32), np.asarray(inp['c'][b], np.float32)], 0)
                             for b in bs], 0)
        m = dict(x_in=np.concatenate([np.asarray(inp['x'][b], np.float32) for b in bs], 0),
                 ctx_in=np.concatenate([np.asarray(inp['ctx'][b], np.float32) for b in bs], 0),
                 cvec=cv, small=small, rpbp=rpbp, consts=cst, rope=rope)
        m.update(wts)
        maps.append(m)
    return maps


_CACHE = {}
ACTIVE = 4


def kernel(**inputs):
    c = make_cfg()
    B = inputs['x'].shape[0]
    NB = B // ACTIVE
    if 'nc' not in _CACHE:
        _CACHE['nc'] = K(c, NB=NB).build()
    nc = _CACHE['nc']
    maps = host_inputs(c, inputs, ACTIVE, NB)
    res = run_bass_kernel_spmd(nc, maps, core_ids=list(range(ACTIVE)))
    out = np.concatenate([res.results[r]['out'].reshape(NB, c['S'], c['D']) for r in range(ACTIVE)], 0)
    return out.astype(np.float32)
```
